# Optimizing a Trainium2 kernel written in Bass

```python
import jax, jax.numpy as jnp
from jax import lax
import numpy as np

D_MODEL = 1024
BATCH = 4
SEQ = 4096
DEPTH = 2
DEC_BATCH = 8
DEC_SEQ = 16
PAST_LEN = 4096

CHUNK = 64
HEAD_DIM = 64
N_Q_HEADS = 8
N_KV_HEADS = 2
Q_PER_KV = N_Q_HEADS // N_KV_HEADS
D_ATTN = N_Q_HEADS * HEAD_DIM
D_KV = N_KV_HEADS * HEAD_DIM
WINDOW = 128
WINDOW_CHUNKS = WINDOW // CHUNK
ROT_DIM = HEAD_DIM // 4
ROPE_THETA = 500000.0
POOL_WINDOWS = (2, 4, 8, 16)
D_POOL = 512
POOL_GROUP = D_POOL // len(POOL_WINDOWS)
POOL_STATE = max(POOL_WINDOWS) - 1
D_MIX = D_ATTN + D_POOL
D_IN = D_ATTN + 2 * D_KV + D_POOL
D_FF = 2816
CONV_W = 3
EPS = 1e-6
NEG_INF = -1e30

kernel_name = "hybrid_swa_sink_pool_convffn_stream_step"


def rms_norm(x, g):
    xf = x.astype(jnp.float32)
    y = xf * lax.rsqrt(jnp.mean(xf * xf, axis=-1, keepdims=True) + EPS)
    return (y * g.astype(jnp.float32)).astype(x.dtype)


def partial_rope(x, pos):
    inv = ROPE_THETA ** (-jnp.arange(0, ROT_DIM, 2, dtype=jnp.float32) / ROT_DIM)
    ang = pos.astype(jnp.float32)[:, None] * inv[None, :]
    cos = jnp.cos(ang)[:, None, :].astype(x.dtype)
    sin = jnp.sin(ang)[:, None, :].astype(x.dtype)
    half = ROT_DIM // 2
    x1 = x[..., :half]
    x2 = x[..., half:ROT_DIM]
    return jnp.concatenate([x1 * cos - x2 * sin, x2 * cos + x1 * sin, x[..., ROT_DIM:]], axis=-1)


def mixer_inputs(xn, pos, w_in, q_gain, k_gain):
    b, l = xn.shape[0], xn.shape[1]
    h = xn @ w_in
    q, k, v, u = jnp.split(h, [D_ATTN, D_ATTN + D_KV, D_ATTN + 2 * D_KV], axis=-1)
    q = q.reshape(b, l, N_Q_HEADS, HEAD_DIM)
    k = k.reshape(b, l, N_KV_HEADS, HEAD_DIM)
    v = v.reshape(b, l, N_KV_HEADS, HEAD_DIM)
    q = partial_rope(rms_norm(q, q_gain), pos)
    k = partial_rope(rms_norm(k, k_gain), pos)
    return q, k, v, u


def sink_attention(q, k, v, sinks, mask):
    lead = q.shape[:-3]
    sq = q.shape[-3]
    qg = q.reshape(*lead, sq, N_KV_HEADS, Q_PER_KV, HEAD_DIM)
    s = jnp.einsum('...qkgd,...skd->...kgqs', qg, k,
                   preferred_element_type=jnp.float32) * (HEAD_DIM ** -0.5)
    if mask is not None:
        s = jnp.where(mask, s, NEG_INF)
    sink = sinks.astype(jnp.float32).reshape(N_KV_HEADS, Q_PER_KV, 1, 1)
    m = jnp.maximum(jnp.max(s, axis=-1, keepdims=True), sink)
    e = jnp.exp(s - m)
    p = e / (jnp.sum(e, axis=-1, keepdims=True) + jnp.exp(sink - m))
    o = jnp.einsum('...kgqs,...skd->...qkgd', p.astype(v.dtype), v)
    return o.reshape(*lead, sq, D_ATTN)


def window_attention_prompt(q, k, v, sinks):
    b, l = q.shape[0], q.shape[1]
    nc = l // CHUNK
    qc = q.reshape(b, nc, CHUNK, N_Q_HEADS, HEAD_DIM)

    def band(t):
        tc = t.reshape(b, nc, CHUNK, N_KV_HEADS, HEAD_DIM)
        tp = jnp.pad(tc, ((0, 0), (WINDOW_CHUNKS, 0), (0, 0), (0, 0), (0, 0)))
        return jnp.concatenate([tp[:, j:j + nc] for j in range(WINDOW_CHUNKS + 1)], axis=2)

    kb = band(k)
    vb = band(v)
    rel = jnp.repeat(jnp.arange(WINDOW_CHUNKS + 1) - WINDOW_CHUNKS, CHUNK)
    key_chunk = jnp.arange(nc)[:, None] + rel[None, :]
    mask = (key_chunk >= 0)[:, None, None, None, :]
    o = sink_attention(qc, kb, vb, sinks, mask)
    return o.reshape(b, l, D_ATTN)


def pool_mixer(u, prev, w_pool, scale):
    full = u if prev is None else jnp.concatenate([prev, u], axis=1)
    l = u.shape[1]
    p0 = full.shape[1] - l
    m = max(POOL_WINDOWS)
    cs = jnp.pad(jnp.cumsum(full.astype(jnp.float32), axis=1), ((0, 0), (m + 1, 0), (0, 0)))
    end = jnp.arange(p0, p0 + l) + 1
    hi = cs[:, m + p0 + 1:m + p0 + l + 1]
    uf = u.astype(jnp.float32)
    outs = []
    for g, w in enumerate(POOL_WINDOWS):
        sl = slice(g * POOL_GROUP, (g + 1) * POOL_GROUP)
        lo = cs[:, m + p0 + 1 - w:m + p0 + l + 1 - w, sl]
        cnt = jnp.minimum(end, w).astype(jnp.float32)[None, :, None]
        d = ((hi[..., sl] - lo) / cnt - uf[..., sl]).astype(u.dtype)
        outs.append(d @ w_pool[g])
    return jnp.concatenate(outs, axis=-1) * scale, full[:, -POOL_STATE:]


def conv_ffn(x, prev, w_up, conv_w, conv_b, w_down):
    h = x @ w_up
    l = h.shape[1]
    if prev is None:
        full = jnp.pad(h, ((0, 0), (CONV_W - 1, 0), (0, 0)))
    else:
        full = jnp.concatenate([prev, h], axis=1)
    c = conv_b
    for j in range(CONV_W):
        c = c + full[:, j:j + l] * conv_w[j]
    gate, val = jnp.split(c, 2, axis=-1)
    y = (jax.nn.silu(gate) * val) @ w_down
    return y, full[:, -(CONV_W - 1):]


def trunk_layer(x, pos, cache_k, cache_v, pool_prev, conv_prev,
                norm_mix, w_in, q_gain, k_gain, sinks, w_pool, pool_scale, w_out,
                norm_ffn, w_up, conv_w, conv_b, w_down):
    xn = rms_norm(x, norm_mix)
    q, k, v, u = mixer_inputs(xn, pos, w_in, q_gain, k_gain)
    if cache_k is None:
        attn = window_attention_prompt(q, k, v, sinks)
        keys, vals = k, v
    else:
        keys = jnp.concatenate([cache_k, k], axis=1)
        vals = jnp.concatenate([cache_v, v], axis=1)
        attn = sink_attention(q, keys, vals, sinks, None)
    pool, new_pool = pool_mixer(u, pool_prev, w_pool, pool_scale)
    x = x + jnp.concatenate([attn, pool], axis=-1) @ w_out
    f, new_conv = conv_ffn(rms_norm(x, norm_ffn), conv_prev, w_up, conv_w, conv_b, w_down)
    x = x + f
    return x, keys[:, -WINDOW:], vals[:, -WINDOW:], new_pool, new_conv


def setup_inputs(seed: int = 0) -> dict:
    key = jax.random.key(seed)
    ks = jax.random.split(key, 20)
    f32 = jnp.float32
    win_buf = min(WINDOW, PAST_LEN)
    nrm = lambda k, shape, s: jax.random.normal(k, shape, f32) * s
    return {
        'x_prompt': nrm(ks[0], (BATCH, SEQ, D_MODEL), 1.0),
        'x_sample': nrm(ks[1], (DEC_BATCH, DEC_SEQ, D_MODEL), 1.0),
        'cache_k': nrm(ks[2], (DEPTH, DEC_BATCH, win_buf, N_KV_HEADS, HEAD_DIM), 1.0),
        'cache_v': nrm(ks[3], (DEPTH, DEC_BATCH, win_buf, N_KV_HEADS, HEAD_DIM), 1.0),
        'state_pool': nrm(ks[4], (DEPTH, DEC_BATCH, POOL_STATE, D_POOL), 1.0),
        'state_conv': nrm(ks[5], (DEPTH, DEC_BATCH, CONV_W - 1, 2 * D_FF), 1.0),
        'norm_mix': 1.0 + nrm(ks[6], (DEPTH, D_MODEL), 0.05),
        'w_in': nrm(ks[7], (DEPTH, D_MODEL, D_IN), D_MODEL ** -0.5),
        'q_norm': 1.0 + nrm(ks[8], (DEPTH, HEAD_DIM), 0.05),
        'k_norm': 1.0 + nrm(ks[9], (DEPTH, HEAD_DIM), 0.05),
        'attn_sinks': nrm(ks[10], (DEPTH, N_Q_HEADS), 0.5),
        'w_pool': nrm(ks[11], (DEPTH, len(POOL_WINDOWS), POOL_GROUP, POOL_GROUP), POOL_GROUP ** -0.5),
        'pool_scale': 1.0 + nrm(ks[12], (DEPTH, D_POOL), 0.1),
        'w_out': nrm(ks[13], (DEPTH, D_MIX, D_MODEL), D_MIX ** -0.5),
        'norm_ffn': 1.0 + nrm(ks[14], (DEPTH, D_MODEL), 0.05),
        'w_up': nrm(ks[15], (DEPTH, D_MODEL, 2 * D_FF), D_MODEL ** -0.5),
        'conv_w': nrm(ks[16], (DEPTH, CONV_W, 2 * D_FF), CONV_W ** -0.5),
        'conv_b': nrm(ks[17], (DEPTH, 2 * D_FF), 0.02),
        'w_down': nrm(ks[18], (DEPTH, D_FF, D_MODEL), D_FF ** -0.5),
    }


def reference(x_prompt, x_sample, cache_k, cache_v, state_pool, state_conv,
              norm_mix, w_in, q_norm, k_norm, attn_sinks, w_pool, pool_scale, w_out,
              norm_ffn, w_up, conv_w, conv_b, w_down):
    pos_p = jnp.arange(x_prompt.shape[1])
    pos_s = PAST_LEN + jnp.arange(x_sample.shape[1])
    yp, ys = x_prompt, x_sample
    kp, vp, pp, cp = [], [], [], []
    kq, vq, pq, cq = [], [], [], []
    for i in range(DEPTH):
        w = (norm_mix[i], w_in[i], q_norm[i], k_norm[i], attn_sinks[i], w_pool[i], pool_scale[i],
             w_out[i], norm_ffn[i], w_up[i], conv_w[i], conv_b[i], w_down[i])
        yp, k1, v1, p1, c1 = trunk_layer(yp, pos_p, None, None, None, None, *w)
        ys, k2, v2, p2, c2 = trunk_layer(ys, pos_s, cache_k[i], cache_v[i], state_pool[i], state_conv[i], *w)
        kp.append(k1); vp.append(v1); pp.append(p1); cp.append(c1)
        kq.append(k2); vq.append(v2); pq.append(p2); cq.append(c2)
    return (yp, ys,
            jnp.stack(kp), jnp.stack(vp), jnp.stack(pp), jnp.stack(cp),
            jnp.stack(kq), jnp.stack(vq), jnp.stack(pq), jnp.stack(cq))
```

```python
import os
import numpy as np
import concourse.bass as bass
import concourse.mybir as mybir
from concourse.bass_utils import run_bass_kernel_spmd

F32 = mybir.dt.float32
BF16 = mybir.dt.bfloat16
AF = mybir.ActivationFunctionType
ALU = mybir.AluOpType

NCORES = 8
TCORE = 2176
W = 384
NTILE = 6
LW = 1184
NX = 416
W5 = 256
S5 = W5 + 16
N5 = W5 + 32
LG = 2 * W + W5
LS = LG + 16
G = 3
RING = 2 * G
GROUPS = [list(range(i, min(i + G, 22))) for i in range(0, 22, G)]
EPS = 1e-6
SM_L = 202
O_G1, O_G2, O_GQ, O_GK, O_SINK, O_PSC, O_CW, O_CB = 0, 8, 16, 17, 18, 22, 26, 158
O_EPS = 404


class Prog:
    def __init__(self, nc):
        self.nc = nc
        self.names = ["pe", "act", "dve", "pool", "sp"]
        self.eng = {"pe": nc.tensor, "act": nc.scalar, "dve": nc.vector, "pool": nc.gpsimd, "sp": nc.sync}
        self.sem = {n: nc.alloc_semaphore("s_" + n) for n in self.names}
        self.cnt = {n: 0 for n in self.names}
        self.lists = {n: [] for n in self.names}
        self.seen = {n: {} for n in self.names}
        self.lastw = {}
        self.readers = {}
        self.ndsem = 56
        self.dsem = [nc.alloc_semaphore("d%d" % i) for i in range(self.ndsem)]
        self.dval = [0] * self.ndsem
        self.dnext = 0
        self.alias = {}

    def _wait(self, eng, tok):
        if tok is None:
            return
        kind, key, val = tok
        if kind == "e" and key == eng and eng in ("pe", "sp"):
            return
        k = (kind, key)
        if self.seen[eng].get(k, 0) >= val:
            return
        self.seen[eng][k] = val
        sem = self.sem[key] if kind == "e" else self.dsem[key]
        self.lists[eng].append(lambda e, sem=sem, val=val: e.wait_ge(sem, val))

    def _canon(self, regs):
        out = []
        for reg in regs:
            out.extend(self.alias.get(reg, [reg]))
        return out

    def _deps(self, eng, r, w, is_dma=False):
        r, w = self._canon(r), self._canon(w)
        for reg in r:
            for tok in self.lastw.get(reg, ()):
                self._wait(eng, tok)
        for reg in w:
            for tok in self.lastw.get(reg, ()):
                if is_dma and tok[0] == "d" and not self.readers.get(reg):
                    continue
                self._wait(eng, tok)
            for tok in self.readers.get(reg, ()):
                self._wait(eng, tok)

    def _commit(self, tok, r, w):
        r, w = self._canon(r), self._canon(w)
        for reg in r:
            self.readers.setdefault(reg, []).append(tok)
        for reg in w:
            prev = self.lastw.get(reg, [])
            if tok[0] == "d" and prev and all(p[0] == "d" for p in prev) and not self.readers.get(reg):
                self.lastw[reg] = prev + [tok]
            else:
                self.lastw[reg] = [tok]
            self.readers[reg] = []

    def op(self, eng, insts, r=(), w=()):
        if isinstance(insts, tuple):
            insts = [insts]
        self._deps(eng, r, w)
        self.cnt[eng] += 1
        sem = self.sem[eng]

        def run(e, insts=insts, sem=sem):
            ins = None
            for meth, kw in insts:
                ins = getattr(e, meth)(**kw)
            ins.then_inc(sem, 1)
        self.lists[eng].append(run)
        self._commit(("e", eng, self.cnt[eng]), r, w)

    def dma(self, q, out, in_, r=(), w=(), **kw):
        self._deps(q, r, w, is_dma=True)
        i = self.dnext
        self.dnext = (self.dnext + 1) % self.ndsem
        if self.dval[i] > 0:
            self._wait(q, ("d", i, self.dval[i]))
        self.dval[i] += 16
        sem = self.dsem[i]
        self.lists[q].append(lambda e, out=out, in_=in_, sem=sem, kw=kw: e.dma_start(out=out, in_=in_, **kw).then_inc(sem, 16))
        self._commit(("d", i, self.dval[i]), r, w)

    def finish(self):
        for i in range(self.ndsem):
            if self.dval[i] > 0:
                self._wait("sp", ("d", i, self.dval[i]))
        for n in self.names:
            if n != "sp" and self.cnt[n] > 0:
                self._wait("sp", ("e", n, self.cnt[n]))
        with self.nc.Block() as block:
            def run(name):
                def f(e):
                    for fn in self.lists[name]:
                        fn(e)
                return f
            block.sync(run("sp"))
            block.tensor(run("pe"))
            block.scalar(run("act"))
            block.vector(run("dve"))
            block.gpsimd(run("pool"))


def build_nc():
    nc = bass.Bass("TRN2", target_bir_lowering=False)
    P = Prog(nc)

    def din(name, shape):
        return nc.dram_tensor(name, list(shape), F32, kind="ExternalInput").ap()

    def dout(name, shape):
        return nc.dram_tensor(name, list(shape), F32, kind="ExternalOutput").ap()

    xp = din("xp", [8, 128, TCORE]); xs = din("xs", [8, 128, 16])
    ck = din("ck", [2, 128, 128]); cv = din("cv", [2, 128, 128])
    spool = din("spool", [2, 15, 512]); sconv = din("sconv", [2, 2, 5632])
    win = din("win", [2, 128, 8, 1280]); wout = din("wout", [2, 128, 8, 1024]); wpool = din("wpool", [2, 128, 4, 128])
    wup = din("wup", [2, 22, 128, 8, 256]); wdn = din("wdn", [2, 22, 128, 1024])
    small = din("small", [128, 405]); constf = din("constf", [128, 192]); constb = din("constb", [128, 448])
    rope = din("rope", [128, 2, NTILE, NX])
    yp = dout("yp", [8, 128, TCORE]); ys = dout("ys", [8, 128, 16])
    okp = dout("okp", [2, 128, 128]); ovp = dout("ovp", [2, 128, 128]); opp = dout("opp", [2, 15, 512]); ocp = dout("ocp", [2, 2, 5632])
    oks = dout("oks", [2, 128, 128]); ovs = dout("ovs", [2, 128, 128]); ops_ = dout("ops", [2, 15, 512]); ocs = dout("ocs", [2, 2, 5632])

    sb = nc.alloc_sbuf_tensor
    XT = sb("XT", [128, 8, LW], F32)
    XN = sb("XN", [128, 8, LW + 2], BF16)
    KT = sb("KT", [128, 2320], BF16)
    VB = sb("VB", [128, 19, 128], BF16)
    KC = sb("KC", [128, 2, 128], BF16); VC = sb("VC", [128, 2, 128], BF16)
    WIN = sb("WIN", [128, 8, 1280], BF16); WOUT = sb("WOUT", [128, 8, 1024], BF16); WPOOL = sb("WPOOL", [128, 4, 128], BF16)
    WUP = sb("WUP", [128, RING, 8, 256], BF16); WDN = sb("WDN", [128, RING, 1024], BF16)
    SMALL = sb("SMALL", [128, 405], F32); CF = sb("CF", [128, 192], F32); CB = sb("CB", [128, 448], BF16)
    ROPE = sb("ROPE", [128, 2, NX], F32)
    U = sb("U", [128, 4, 15 + NX], F32)
    FS = [sb("FS%d" % i, [128, 432], F32) for i in range(8)]
    BQ = [sb("BQ%d" % i, [128, 4, NX], BF16) for i in range(4)]
    BS = [sb("BS%d" % i, [128, NX], BF16) for i in range(6)]
    PAB = sb("PAB", [128, 2, 2, 512], BF16)
    RD = sb("RD", [128, 512], F32)
    STG = [sb("STG%d" % i, [128, 1024], F32) for i in range(2)]
    ESINK = sb("ESINK", [128, 8], F32)
    UC = sb("UC", [128, 2, 4, 15], F32); CARRY = sb("CARRY", [128, 2, 8, 2], BF16)
    SPT = sb("SPT", [128, 2, 4, 15], F32); SCT = sb("SCT", [128, 2, 2, 44], F32); CORR = sb("CORR", [128, 2, 44, 2], F32)
    HL = sb("HL", [128, 44, 4], F32); TMPC = sb("TMPC", [128, 44], F32)
    CKT = sb("CKT", [128, 2, 128], BF16); CV = sb("CV", [128, 2, 128], BF16)
    PS = [nc.alloc_psum_tensor("PS%d" % i, [128, 512], F32) for i in range(8)]

    IDENT = CF[:, 0:128]
    INVC = CF[:, 128:192]
    ONES_D = CB[:, 0:128]; BLK = CB[:, 128:256]; PERM = CB[:, 256:384]; ONESK = CB[:, 384:448]
    EPSC = SMALL[:, O_EPS:O_EPS + 1]

    def sm(l, off, n=1):
        return SMALL[:, l * SM_L + off: l * SM_L + off + n]

    rot = {}

    def nxt(key, choices):
        v = choices[rot.get(key, 0) % len(choices)]
        rot[key] = rot.get(key, 0) + 1
        return v

    def MM(out, lhsT, rhs, start=True, stop=True, tp=None):
        kw = dict(out=out, lhsT=lhsT, rhs=rhs, start=start, stop=stop)
        if tp is not None:
            kw["tile_position"] = tp
        return ("matmul", kw)

    def TR(out, in_, ident):
        return ("transpose", dict(out=out, in_=in_, identity=ident))

    def ACT(out, in_, func, bias=None, scale=None):
        kw = dict(out=out, in_=in_, func=func)
        if bias is not None:
            kw["bias"] = bias
        if scale is not None:
            kw["scale"] = scale
        return ("activation", kw)

    def TT(out, in0, in1, op):
        return ("tensor_tensor", dict(out=out, in0=in0, in1=in1, op=op))

    def STT(out, in0, scalar, in1, op0, op1):
        return ("scalar_tensor_tensor", dict(out=out, in0=in0, scalar=scalar, in1=in1, op0=op0, op1=op1))

    def CP(out, in_):
        return ("tensor_copy", dict(out=out, in_=in_))

    def MS(ap, v):
        return ("memset", dict(ap=ap, constant=v))

    def RCP(out, in_):
        return ("reciprocal", dict(out=out, in_=in_))

    def v4(ap):
        return ap.rearrange("p (a b) -> p a b", a=4)

    FSN = list(range(8))

    P.dma("sp", SMALL[:], small[:, :], w=["SMALL"])
    for h in range(4):
        P.dma("act", XT[:, 2 * h:2 * h + 2, 0:W], xp[2 * h:2 * h + 2, :, 0:W].rearrange("k p t -> p k t"),
              w=[("XT", 0, kc) for kc in range(2 * h, 2 * h + 2)])
    P.dma("sp", CF[:], constf[:, :], w=["CF"])
    P.dma("pool", CB[:], constb[:, :], w=["CB"])
    for l in range(2):
        P.dma("pool", CV[:, l, :], cv[l, :, :], w=["CV"])
    P.op("dve", MS(PAB[:], 0.0), w=["PA0", "PA1", "PB0", "PB1"])
    P.op("dve", MS(XN[:, :, 0:2], 0.0), w=["XNc"])
    P.op("dve", MS(UC[:], 0.0), w=[("UC", l, g) for l in range(2) for g in range(4)])
    P.alias[("ps", 6)] = [("ps", 6, 0), ("ps", 6, 1)]
    P.alias[("ps", 7)] = [("ps", 7, 0), ("ps", 7, 1)]
    P.alias["BQM1"] = ["AT"]
    P.alias["BQM2"] = [("D", g) for g in range(4)]
    bq_regions = [[("QT", j) for j in range(4)], ["AT"], [("D", g) for g in range(4)], [("PL", g) for g in range(4)]]
    for i in range(4):
        P.op("dve", MS(BQ[i][:], 0.0), w=bq_regions[i])
    P.op("dve", MS(U[:], 0.0), w=[("U", g) for g in range(4)] + [("Uc", g) for g in range(4)])
    P.op("dve", MS(HL[:], 0.0), w=["HL"])
    for l in range(2):
        P.op("act", ACT(ESINK[:, 4 * l:4 * l + 4], sm(l, O_SINK, 4), AF.Exp), r=["SMALL"], w=["ESINK"])
    for l in range(2):
        stg = nxt("stg", [0, 1])
        P.dma("sp", STG[stg][0:15, 0:512], spool[l, :, :], w=["STG%d" % stg])
        P.op("pe", [TR(PS[0][:, g * 16:g * 16 + 15], STG[stg][0:15, g * 128:(g + 1) * 128], CF[0:15, 0:15]) for g in range(4)],
             r=["STG%d" % stg, "CF"], w=[("ps", 0)])
        P.op("act", ACT(SPT[:, l, :, :], PS[0][:, 0:64].rearrange("p (a b) -> p a b", a=4)[:, :, 0:15], AF.Copy), r=[("ps", 0)], w=["SPT"])
        stg = nxt("stg", [0, 1])
        P.dma("sp", STG[stg][0:88, 0:128], sconv[l, :, :].rearrange("r (c p) -> (r c) p", p=128), w=["STG%d" % stg])
        P.op("pe", TR(PS[1][:, 0:88], STG[stg][0:88, 0:128], CF[0:88, 0:88]), r=["STG%d" % stg, "CF"], w=[("ps", 1)])
        P.op("act", ACT(SCT[:, l, :, :].rearrange("p a b -> p (a b)"), PS[1][:, 0:88], AF.Copy), r=[("ps", 1)], w=["SCT"])
    for l in range(2):
        cw0 = sm(l, O_CW, 44)
        cw1 = sm(l, O_CW + 44, 44)
        P.op("dve", TT(CORR[:, l, :, 1], cw0, SCT[:, l, 1, :], ALU.mult), r=["SMALL", "SCT"], w=["CORR"])
        P.op("dve", TT(TMPC[:], cw1, SCT[:, l, 1, :], ALU.mult), r=["SMALL", "SCT"], w=["TMPC"])
        P.op("dve", TT(CORR[:, l, :, 0], cw0, SCT[:, l, 0, :], ALU.mult), r=["SMALL", "SCT"], w=["CORR"])
        P.op("dve", TT(CORR[:, l, :, 0], CORR[:, l, :, 0], TMPC[:], ALU.add), r=["TMPC", "CORR"], w=["CORR"])

    def load_mixer_weights(l, gate=False):
        after = ["WIN"] if gate else []
        for h in range(4):
            P.dma("pool", WIN[:, 2 * h:2 * h + 2, :], win[l, :, 2 * h:2 * h + 2, :], w=["WIN"])
        for h in range(2):
            P.dma("pool", WOUT[:, 4 * h:4 * h + 4, :], wout[l, :, 4 * h:4 * h + 4, :], r=after, w=["WOUT"])
        P.dma("pool", WPOOL[:], wpool[l, :, :, :], r=after, w=["WPOOL"])

    slab_q = {"issued": 0}
    passes = [(st, l) for st in range(2) for l in range(2)]

    def issue_slabs(upto):
        while slab_q["issued"] < min(upto, len(passes) * 22):
            q = slab_q["issued"]
            l = passes[q // 22][1]
            s = q % 22
            slot = q % RING
            P.dma("pool", WUP[:, slot, :, :], wup[l, s, :, :, :], w=[("WS", slot)])
            P.dma("pool", WDN[:, slot, :], wdn[l, s, :, :], w=[("WS", slot)])
            slab_q["issued"] += 1

    def tile_info(t):
        st, lt = divmod(t, 3)
        N = N5 if t == 5 else W
        return st, lt, lt * W, N

    def nsub(t):
        return 2 if t == 5 else 3

    class NoRes(Exception):
        pass

    free_lists = {"st": [3, 4, 6, 7], "proj": [0, 1, 2], "nb": [5], "bs": list(range(len(BS))), "fs": list(range(8)), "stg": [0, 1]}

    class Chain:
        def __init__(self):
            self.items = []
            self.owned = []

        def op(self, eng, insts, r=(), w=()):
            self.items.append(("op", (eng, insts), dict(r=list(r), w=list(w))))

        def dma(self, q, out, in_, r=(), w=(), **kw):
            self.items.append(("dma", (q, out, in_), dict(r=list(r), w=list(w), **kw)))

        def alloc(self, kind):
            fl = free_lists[kind]
            if not fl:
                raise NoRes(kind)
            x = fl.pop(0)
            self.owned.append((kind, x))
            return x

        def free(self, kind, x):
            self.items.append(("free", (kind, x), {}))

        def peek(self, kind):
            fl = free_lists[kind]
            if not fl:
                raise NoRes(kind)
            return fl[0]

        def hold(self):
            self.items.append(("hold", (), {}))

        def unhold(self):
            self.items.append(("unhold", (), {}))

    sim = {"eng": {n: 0.0 for n in P.names}, "now": 0.0, "reg": {}}

    def regs_ready(kw):
        t_ = 0.0
        for reg in P._canon(list(kw.get("r", ())) + list(kw.get("w", ()))):
            t_ = max(t_, sim["reg"].get(reg, 0.0))
        return t_

    def op_cost(kind, a):
        if kind == "dma":
            return 60.0, 2000.0
        eng, insts = a
        if isinstance(insts, tuple):
            insts = [insts]
        tot = 0.0
        for meth, kw in insts:
            o = kw.get("out", kw.get("ap"))
            n = 1
            for d_ in o.shape[1:]:
                n *= d_
            if eng == "pe":
                tot += 20 + n / (1.2 if "tile_position" in kw else 1.9)
            elif eng == "act":
                tot += 150 + 0.8 * n
            elif eng == "dve":
                tot += 80 + 1.15 * n
            else:
                tot += 50 + 2.1 * n
        return tot, tot + 100.0

    def flush(factories, width=3):
        if os.environ.get("K_NOILV") is not None:
            width = 1
        pending = list(factories)
        active = []
        pos = {}
        ready = {}

        def release(c, kind, x):
            c.owned.remove((kind, x))
            free_lists[kind].append(x)

        def try_start():
            while pending and len(active) < width:
                c = Chain()
                try:
                    pending[0](c)
                except NoRes:
                    for kind, x in c.owned:
                        free_lists[kind].insert(0, x)
                    if not active:
                        raise
                    return
                pending.pop(0)
                if c.items:
                    active.append(c)
                    pos[id(c)] = 0
                    ready[id(c)] = sim["now"]

        def head(c):
            kind, a, kw = c.items[pos[id(c)]]
            return kind, a, kw

        def emit_one(c):
            kind, a, kw = head(c)
            eng = a[0]
            busy, lat = op_cost(kind, a)
            start = max(sim["eng"][eng], ready[id(c)], regs_ready(kw))
            (P.op if kind == "op" else P.dma)(*a, **kw)
            sim["eng"][eng] = start + busy
            ready[id(c)] = start + lat
            for reg in P._canon(list(kw.get("w", ()))):
                sim["reg"][reg] = start + lat
            sim["now"] = max(sim["now"], start)
            pos[id(c)] += 1
            return True

        def skip_markers(c, held):
            while pos[id(c)] < len(c.items):
                kind, a, kw = c.items[pos[id(c)]]
                if kind == "free":
                    release(c, *a)
                elif kind == "hold":
                    held = True
                elif kind == "unhold":
                    held = False
                else:
                    return True, held
                pos[id(c)] += 1
            active.remove(c)
            for kind2, x in list(c.owned):
                release(c, kind2, x)
            return False, False

        def start_chains():
            n0 = len(active)
            try_start()
            for c in list(active[n0:]):
                skip_markers(c, False)

        start_chains()
        while active:
            best, best_t = None, None
            for c in active:
                kind, a, kw = head(c)
                t_ = max(sim["eng"][a[0]], ready[id(c)], regs_ready(kw))
                if best is None or t_ < best_t:
                    best, best_t = c, t_
            c = best
            held = False
            while True:
                emit_one(c)
                alive, held = skip_markers(c, held)
                if not alive or not held:
                    break
            start_chains()

    def load_x(t, queues=("sp",), after=()):
        if True:
            _, lt, lc0, N = tile_info(t)
            wt = nsub(t) * 128
            for h in range(4):
                P.dma(queues[h % len(queues)], XT[:, 2 * h:2 * h + 2, lc0:lc0 + wt], xp[2 * h:2 * h + 2, :, t * W:t * W + wt].rearrange("k p t -> p k t"),
                      r=list(after), w=[("XT", lt, kc) for kc in range(2 * h, 2 * h + 2)])
            if t == 5:
                P.dma("sp", XT[:, :, LS:LS + 16], xs[:, :, :].rearrange("k p t -> p k t"), w=[("XT", 2, kc) for kc in range(8)])

    def ch_rmsnorm(C, t, gcol):
        st, lt, lc0, N = tile_info(t)
        nb = C.alloc("nb")
        sqs = [C.alloc("bs"), C.alloc("bs")]
        fs = C.alloc("fs")
        for kc in range(8):
            sq = sqs[kc % 2]
            C.op("act", ACT(BS[sq][:, 0:N], XT[:, kc, lc0:lc0 + N], AF.Square), r=[("XT", lt, kc)], w=["BS%d" % sq])
            C.op("pe", MM(PS[nb][:, 0:N], ONES_D, BS[sq][:, 0:N], start=(kc == 0), stop=(kc == 7)), r=["BS%d" % sq, "CB"], w=[("ps", nb)])
        C.free("bs", sqs[0])
        C.free("bs", sqs[1])
        C.op("act", ACT(FS[fs][:, 0:N], PS[nb][:, 0:N], AF.Ln, bias=EPSC, scale=1.0), r=[("ps", nb), "SMALL"], w=["FS%d" % fs])
        C.op("act", ACT(FS[fs][:, 0:N], FS[fs][:, 0:N], AF.Exp, scale=-0.5), r=["FS%d" % fs], w=["FS%d" % fs])
        for kc in range(8):
            C.op("dve", STT(XN[:, kc, 2 + lc0:2 + lc0 + N], XT[:, kc, lc0:lc0 + N], SMALL[:, gcol + kc:gcol + kc + 1], FS[fs][:, 0:N], ALU.mult, ALU.mult),
                 r=[("XT", lt, kc), "FS%d" % fs, "SMALL"], w=[("XN", lt)])
        if t == 5:
            C.op("dve", MS(XN[:, :, 2 + LG:2 + LS], 0.0), w=[("XN", 2)])

    def proj_op(C, t, col0, b):
        st, lt, lc0, N = tile_info(t)
        C.op("pe", [MM(PS[b][:, 0:N], WIN[:, kc, col0:col0 + 128], XN[:, kc, 2 + lc0:2 + lc0 + N], start=(kc == 0), stop=(kc == 7)) for kc in range(8)],
             r=["WIN", ("XN", lt)], w=[("ps", b)])

    def ch_qk(C, t, l, j):
        st, lt, lc0, N = tile_info(t)
        b = C.alloc("proj")
        sq = C.alloc("bs")
        qg = C.alloc("bs")
        fr = C.alloc("fs")
        f1 = C.alloc("fs")
        f2 = C.alloc("fs")
        mb = C.alloc("st")
        wb = C.alloc("st")
        if j == 4 and t == 5:
            b2 = C.alloc("proj")
            stgs = [C.alloc("stg"), C.alloc("stg")]
        proj_op(C, t, j * 128, b)
        gcol = sm(l, O_GQ if j < 4 else O_GK)
        C.op("act", ACT(BS[sq][:, 0:N], PS[b][:, 0:N], AF.Square), r=[("ps", b)], w=["BS%d" % sq])
        C.op("act", ACT(BS[qg][:, 0:N], PS[b][:, 0:N], AF.Identity, scale=gcol), r=[("ps", b), "SMALL"], w=["BS%d" % qg])
        C.free("proj", b)
        C.op("pe", MM(PS[mb][:, 0:N], BLK, BS[sq][:, 0:N]), r=["BS%d" % sq, "CB"], w=[("ps", mb)])
        C.free("bs", sq)
        C.op("pe", MM(PS[wb][:, 0:N], PERM, BS[qg][:, 0:N]), r=["BS%d" % qg, "CB"], w=[("ps", wb)])
        C.op("act", ACT(FS[fr][:, 0:N], PS[mb][:, 0:N], AF.Ln, bias=EPSC, scale=1.0), r=[("ps", mb), "SMALL"], w=["FS%d" % fr])
        C.op("act", ACT(FS[fr][:, 0:N], FS[fr][:, 0:N], AF.Exp, scale=-0.5), r=["FS%d" % fr], w=["FS%d" % fr])
        C.free("st", mb)
        C.op("dve", TT(FS[f1][:, 0:N], PS[wb][:, 0:N], ROPE[:, 1, 0:N], ALU.mult), r=[("ps", wb), "ROPE"], w=["FS%d" % f1])
        C.free("st", wb)
        C.op("dve", TT(FS[f2][:, 0:N], BS[qg][:, 0:N], ROPE[:, 0, 0:N], ALU.mult), r=["BS%d" % qg, "ROPE"], w=["FS%d" % f2])
        C.free("bs", qg)
        C.op("pool", TT(FS[f1][:, 0:N], FS[f1][:, 0:N], FS[f2][:, 0:N], ALU.add), r=["FS%d" % f1, "FS%d" % f2], w=["FS%d" % f1])
        if j < 4:
            C.op("dve", TT(BQ[0][:, j, 0:N], FS[f1][:, 0:N], FS[fr][:, 0:N], ALU.mult), r=["FS%d" % f1, "FS%d" % fr], w=[("QT", j)])
            return
        C.op("dve", TT(FS[f1][:, 0:N], FS[f1][:, 0:N], FS[fr][:, 0:N], ALU.mult), r=["FS%d" % f1, "FS%d" % fr], w=["FS%d" % f1])
        cc0 = t * W
        wk = W5 if t == 5 else W
        C.op("act", ACT(KT[:, cc0:cc0 + wk], FS[f1][:, 0:wk], AF.Copy), r=["FS%d" % f1], w=["KT"])
        if t == 5:
            C.op("act", ACT(KT[:, 2304:2320], FS[f1][:, S5:S5 + 16], AF.Copy), r=["FS%d" % f1], w=["KT"])
            for i_, (src, nrows, dst) in enumerate(((FS[f1][:, W5 - 128:W5], 128, okp[l, :, :]), (FS[f1][:, S5:S5 + 16], 16, oks[l, 112:128, :]))):
                stg = stgs[i_]
                C.op("pe", TR(PS[b2][0:nrows, 0:128], src, IDENT), r=["FS%d" % f1, "CF"], w=[("ps", b2)])
                C.op("act", ACT(STG[stg][0:nrows, 0:128], PS[b2][0:nrows, 0:128], AF.Copy), r=[("ps", b2)], w=["STG%d" % stg])
                C.dma("sp", dst, STG[stg][0:nrows, 0:128], r=["STG%d" % stg])
            C.dma("sp", oks[l, 0:112, :], ck[l, 16:128, :])

    def ch_u(C, t, l, g):
        st, lt, lc0, N = tile_info(t)
        b = C.alloc("proj")
        C.op("dve", CP(U[:, g, 0:15], UC[:, l, g, :]), r=[("UC", l, g)], w=[("Uc", g)])
        proj_op(C, t, 640 + g * 128, b)
        C.op("act", ACT(U[:, g, 15:15 + N], PS[b][:, 0:N], AF.Copy), r=[("ps", b)], w=[("U", g)])
        if t == 5:
            C.op("dve", CP(U[:, g, S5:S5 + 15], SPT[:, l, g, :]), r=["SPT"], w=[("U", g)])
        C.op("dve", CP(UC[:, l, g, :], U[:, g, W:W + 15]), r=[("U", g)], w=[("UC", l, g)])

    def ch_v(C, t, l, sub):
        st, lt, lc0, N = tile_info(t)
        ns = nsub(t)
        m, c0, slot = (128, lc0 + sub * 128, t * 3 + sub) if sub < ns else (16, LS, 18)
        b = C.alloc("proj")
        if t == 5 and sub >= ns - 1:
            stg = C.alloc("stg")
        C.op("pe", [MM(PS[b][0:m, 0:128], XN[:, kc, 2 + c0:2 + c0 + m], WIN[:, kc, 1152:1280], start=(kc == 0), stop=(kc == 7)) for kc in range(8)],
             r=["WIN", ("XN", lt)], w=[("ps", b)])
        C.op("act", ACT(VB[0:m, slot, :], PS[b][0:m, 0:128], AF.Copy), r=[("ps", b)], w=[("VB", slot)])
        if t == 5 and sub >= ns - 1:
            C.op("act", ACT(STG[stg][0:m, 0:128], PS[b][0:m, 0:128], AF.Copy), r=[("ps", b)], w=["STG%d" % stg])
            if sub == ns - 1:
                C.dma("sp", ovp[l, :, :], STG[stg][0:128, 0:128], r=["STG%d" % stg])
            else:
                C.dma("sp", ovs[l, 112:128, :], STG[stg][0:16, 0:128], r=["STG%d" % stg])
                C.dma("sp", ovs[l, 0:112, :], cv[l, 16:128, :])

    QTALL = [("QT", j) for j in range(4)]

    def attn_unit(t, l, pi, g, sbank):
        QT = BQ[0]
        Pg = t * 3 + pi
        q0 = pi * 128
        hasA = Pg > 0
        pr = slice(g * 64, (g + 1) * 64)
        qrhs = QT[pr, :, q0:q0 + 128]
        tpk = (g * 64, 0)
        tpo = (0, g * 64)
        pa = v4(PAB[:, g, 0, :])
        pb = v4(PAB[:, g, 1, :])
        bA, bB = sbank
        sa = v4(PS[bA][:, :])
        sB = v4(PS[bB][:, :])
        vA = [("VB", Pg - 1)] if hasA else []
        S, E, V = [], [], []
        if hasA:
            S.append(("pe", MM(PS[bA][:, 0:512], KT[pr, (Pg - 1) * 128:Pg * 128], qrhs, tp=tpk), ["KT"] + QTALL, [("ps", bA)]))
        S.append(("pe", MM(PS[bB][:, 0:512], KT[pr, Pg * 128:(Pg + 1) * 128], qrhs, tp=tpk), ["KT"] + QTALL, [("ps", bB)]))
        if hasA:
            E.append(("act", [ACT(pa[:, :, 0:64], sa[:, :, 0:64], AF.Exp, scale=0.125),
                              ACT(pa[64:128, :, 64:128], sa[64:128, :, 64:128], AF.Exp, scale=0.125)], [("ps", bA)], ["PA%d" % g]))
        E.append(("act", [ACT(pb[0:64, :, 0:64], sB[0:64, :, 0:64], AF.Exp, scale=0.125),
                          ACT(pb[:, :, 64:128], sB[:, :, 64:128], AF.Exp, scale=0.125)], [("ps", bB)], ["PB%d" % g]))
        ins = []
        if hasA:
            ins.append(MM(PS[6][pr, 0:512], VB[:, Pg - 1, pr], PAB[:, g, 0, :], start=True, stop=False, tp=tpo))
        ins.append(MM(PS[6][pr, 0:512], VB[:, Pg, pr], PAB[:, g, 1, :], start=(not hasA), stop=True, tp=tpo))
        if hasA:
            ins.append(MM(PS[7][pr, 0:512], ONESK, PAB[:, g, 0, :], start=True, stop=False, tp=tpo))
        ins.append(MM(PS[7][pr, 0:512], ONESK, PAB[:, g, 1, :], start=(not hasA), stop=True, tp=tpo))
        V.append(("pe", ins, vA + [("VB", Pg), "CB", "PA%d" % g, "PB%d" % g], [("ps", 6, g), ("ps", 7, g)]))
        return S, E, V

    def ch_attn_norm(C, t, l, pi):
        AT = BQ[1]
        es = ESINK[:, 4 * l:4 * l + 4]
        q0 = pi * 128
        both6 = [("ps", 6, 0), ("ps", 6, 1)]
        both7 = [("ps", 7, 0), ("ps", 7, 1)]
        C.op("dve", TT(v4(RD[:, :]), v4(PS[7][:, :]), es.unsqueeze(2).to_broadcast([128, 4, 128]), ALU.add), r=both7 + ["ESINK"], w=["RD"])
        C.op("act", ACT(RD[:, :], RD[:, :], AF.Ln), r=["RD"], w=["RD"])
        C.op("act", ACT(RD[:, :], RD[:, :], AF.Exp, scale=-1.0), r=["RD"], w=["RD"])
        C.op("dve", TT(AT[:, :, q0:q0 + 128], v4(PS[6][:, :]), v4(RD[:, :]), ALU.mult), r=both6 + ["RD"], w=["AT"])

    def ch_attn_sample(C, t, l):
        QT, AT = BQ[0], BQ[1]
        es = ESINK[:, 4 * l:4 * l + 4]
        v16 = lambda ap: ap.rearrange("p (a b) -> p a b", a=4)
        for g in range(2):
            pr = slice(g * 64, (g + 1) * 64)
            qrhs = QT[pr, :, S5:S5 + 16]
            tpk = (g * 64, 0)
            tpo = (0, g * 64)
            C.op("pe", MM(PS[3][:, 0:64], CKT[pr, l, :], qrhs, tp=tpk), r=["CKT"] + QTALL, w=[("ps", 3)])
            C.op("pe", MM(PS[4][0:16, 0:64], KT[pr, 2304:2320], qrhs, tp=tpk), r=["KT"] + QTALL, w=[("ps", 4)])
            C.op("act", ACT(PAB[:, g, 0, 0:64], PS[3][:, 0:64], AF.Exp, scale=0.125), r=[("ps", 3)], w=["PA%d" % g])
            C.op("act", ACT(PAB[0:16, g, 1, 0:64], PS[4][0:16, 0:64], AF.Exp, scale=0.125), r=[("ps", 4)], w=["PB%d" % g])
            ins = [MM(PS[6][pr, 0:64], CV[:, l, pr], PAB[:, g, 0, 0:64], start=True, stop=False, tp=tpo),
                   MM(PS[6][pr, 0:64], VB[0:16, 18, pr], PAB[0:16, g, 1, 0:64], start=False, stop=True, tp=tpo),
                   MM(PS[7][pr, 0:64], ONESK, PAB[:, g, 0, 0:64], start=True, stop=False, tp=tpo),
                   MM(PS[7][pr, 0:64], ONESK[0:16, :], PAB[0:16, g, 1, 0:64], start=False, stop=True, tp=tpo)]
            C.op("pe", ins, r=[("VB", 18), "CV", "CB", "PA%d" % g, "PB%d" % g], w=[("ps", 6, g), ("ps", 7, g)])
        C.op("dve", TT(v16(RD[:, 0:64]), v16(PS[7][:, 0:64]), es.unsqueeze(2).to_broadcast([128, 4, 16]), ALU.add),
             r=[("ps", 7, 0), ("ps", 7, 1), "ESINK"], w=["RD"])
        C.op("act", ACT(RD[:, 0:64], RD[:, 0:64], AF.Ln), r=["RD"], w=["RD"])
        C.op("act", ACT(RD[:, 0:64], RD[:, 0:64], AF.Exp, scale=-1.0), r=["RD"], w=["RD"])
        C.op("dve", TT(AT[:, :, S5:S5 + 16], v16(PS[6][:, 0:64]), v16(RD[:, 0:64]), ALU.mult), r=[("ps", 6, 0), ("ps", 6, 1), "RD"], w=["AT"])

    def ch_attention(C, t, l):
        got = sorted(C.alloc("st") for _ in range(4))
        assert got == [3, 4, 6, 7], got
        setB = (C.alloc("proj"), C.alloc("proj"))
        sets = [(3, 4), setB]
        units = [attn_unit(t, l, pi, g, sets[g]) for pi in range(nsub(t)) for g in range(2)]

        def emit(lst):
            for eng, insts, r, w in lst:
                C.op(eng, insts, r=r, w=w)
        emit(units[0][0])
        for u in range(len(units)):
            if u + 1 < len(units):
                emit(units[u + 1][0])
            emit(units[u][1])
            if u % 2 == 0 and u >= 2:
                ch_attn_norm(C, t, l, u // 2 - 1)
            emit(units[u][2])
        ch_attn_norm(C, t, l, len(units) // 2 - 1)
        if t == 5:
            ch_attn_sample(C, t, l)

    def ch_pools(C, t, l):
        st, lt, lc0, N = tile_info(t)
        E = 15 + N
        D, PL = BQ[2], BQ[3]
        fss = [C.alloc("fs"), C.alloc("fs"), C.alloc("fs")]
        b = C.peek("proj")
        if t == 5:
            stg = C.alloc("stg")
        for g in range(4):
            bufs = fss[0:2]
            src = None
            for k in range(1, g + 2):
                s_ = 1 << (k - 1)
                lo = (1 << k) - 1
                dst = bufs[(k - 1) % 2]
                if k == 1:
                    C.op("pool", TT(FS[dst][:, lo:E], U[:, g, lo:E], U[:, g, lo - s_:E - s_], ALU.add), r=[("U", g), ("Uc", g)], w=["FS%d" % dst])
                else:
                    C.op("pool", TT(FS[dst][:, lo:E], FS[src][:, lo:E], FS[src][:, lo - s_:E - s_], ALU.add), r=["FS%d" % src], w=["FS%d" % dst])
                src = dst
            wv = float(1 << (g + 1))
            C.op("dve", STT(D[:, g, 0:N], FS[src][:, 15:E], 1.0 / wv, U[:, g, 15:E], ALU.mult, ALU.subtract), r=["FS%d" % src, ("U", g)], w=[("D", g)])
            if t == 0:
                f3 = fss[2]
                C.op("dve", TT(FS[f3][:, 0:16], FS[src][:, 15:31], INVC[:, g * 16:(g + 1) * 16], ALU.mult), r=["FS%d" % src, "CF"], w=["FS%d" % f3])
                C.op("dve", TT(D[:, g, 0:16], FS[f3][:, 0:16], U[:, g, 15:31], ALU.subtract), r=["FS%d" % f3, ("U", g)], w=[("D", g)])
            C.hold()
            C.op("pe", MM(PS[b][:, 0:N], WPOOL[:, g, :], D[:, g, 0:N]), r=["WPOOL", ("D", g)], w=[("ps", b)])
            C.op("act", ACT(PL[:, g, 0:N], PS[b][:, 0:N], AF.Identity, scale=sm(l, O_PSC + g)), r=[("ps", b), "SMALL"], w=[("PL", g)])
            C.unhold()
        if t == 5:
            for c0, dst_ in ((W5, opp), (S5 + 16, ops_)):
                C.hold()
                C.op("pe", [TR(PS[b][0:15, gg * 128:(gg + 1) * 128], U[:, gg, c0:c0 + 15], IDENT) for gg in range(4)],
                     r=[("U", gg) for gg in range(4)] + ["CF"], w=[("ps", b)])
                C.op("act", ACT(STG[stg][0:15, 0:512], PS[b][0:15, 0:512], AF.Copy), r=[("ps", b)], w=["STG%d" % stg])
                C.unhold()
                C.dma("sp", dst_[l, :, :], STG[stg][0:15, 0:512], r=["STG%d" % stg])

    def ch_wout(C, t, mo):
        st, lt, lc0, N = tile_info(t)
        AT, PL = BQ[1], BQ[3]
        b = C.alloc("proj")
        C.op("pe", [MM(PS[b][:, 0:N], WOUT[:, kc, mo * 128:(mo + 1) * 128], (AT[:, kc, 0:N] if kc < 4 else PL[:, kc - 4, 0:N]),
                       start=(kc == 0), stop=(kc == 7)) for kc in range(8)], r=["WOUT", "AT"] + [("PL", g) for g in range(4)], w=[("ps", b)])
        C.op("dve", TT(XT[:, mo, lc0:lc0 + N], PS[b][:, 0:N], XT[:, mo, lc0:lc0 + N], ALU.add), r=[("ps", b), ("XT", lt, mo)], w=[("XT", lt, mo)])

    def ch_prologue(C, t, l):
        ch_rmsnorm(C, t, l * SM_L + O_G1)

    def mixer_tile(t, l, prologue_done, next_t, extra_tail=(), defer_wout=False, pre_wouts=()):
        st, lt, lc0, N = tile_info(t)
        if not prologue_done:
            flush([lambda C: ch_prologue(C, t, l)])
        P.dma("sp", ROPE[:, :, 0:N], rope[:, :, t, 0:N], w=["ROPE"])
        F = lambda fn, *a: (lambda C: fn(C, *a))
        vs = [F(ch_v, t, l, sub) for sub in range(nsub(t) + (1 if t == 5 else 0))]
        us = [F(ch_u, t, l, g) for g in range(4)]
        qs = [F(ch_qk, t, l, j) for j in (4, 0, 1, 2, 3)]
        order = [qs[0], qs[1], vs[0], qs[2], vs[1], qs[3], vs[2], qs[4]] + vs[3:] + [us[0], us[1], us[2], us[3]]
        if pre_wouts:
            merged, pw = [], list(pre_wouts)
            for ch_ in order:
                merged.append(ch_)
                if pw:
                    merged.append(pw.pop(0))
            order = merged + pw
        flush(order, width=4)
        tail = [F(ch_attention, t, l), F(ch_pools, t, l)]
        if next_t is not None:
            tail.append(F(ch_prologue, next_t, l))
        tail += list(extra_tail)
        flush(tail, width=3)
        wouts = [F(ch_wout, t, mo) for mo in range(8)]
        if defer_wout:
            return wouts
        flush(wouts, width=2)

    def out_y(t):
        st, lt, lc0, N = tile_info(t)
        wt = nsub(t) * 128
        for h in range(4):
            P.dma("sp", yp[2 * h:2 * h + 2, :, t * W:t * W + wt].rearrange("k p t -> p k t"), XT[:, 2 * h:2 * h + 2, lc0:lc0 + wt],
                  r=[("XT", lt, kc) for kc in range(2 * h, 2 * h + 2)])
        if t == 5:
            P.dma("sp", ys[:, :, :].rearrange("k p t -> p k t"), XT[:, :, LS:LS + 16], r=[("XT", 2, kc) for kc in range(8)])
        if t < 3:
            load_x(t + 3)

    def ffn_up(C, t, l, grp, slots, mi, only=None):
        _, lt, lc0, N = tile_info(t)
        M = BQ[mi]
        Mname = "BQM%d" % mi
        for si, s in enumerate(grp):
            if only is not None and si not in only:
                continue
            slot = slots[si]
            res = []
            for half in range(2):
                hb = nxt("h", [0, 1, 2, 3])
                c = s + 22 * half
                C.op("pe", [MM(PS[hb][:, 0:N + 2], WUP[:, slot, kc, half * 128:(half + 1) * 128], XN[:, kc, lc0:lc0 + N + 2],
                               start=(kc == 0), stop=(kc == 7)) for kc in range(8)],
                     r=[("WS", slot), ("XN", lt), ("XN", max(lt - 1, 0)), "XNc"], w=[("ps", hb)])
                a = nxt("fs", FSN)
                C.op("act", ACT(FS[a][:, 0:N], PS[hb][:, 2:N + 2], AF.Identity, bias=sm(l, O_CB + c), scale=sm(l, O_CW + 88 + c)),
                     r=[("ps", hb), "SMALL"], w=["FS%d" % a])
                for j in (1, 0):
                    C.op("dve", STT(FS[a][:, 0:N], PS[hb][:, j:N + j], sm(l, O_CW + 44 * j + c), FS[a][:, 0:N], ALU.mult, ALU.add),
                         r=[("ps", hb), "FS%d" % a, "SMALL"], w=["FS%d" % a])
                if t == 5:
                    C.op("dve", TT(FS[a][:, S5:S5 + 2], FS[a][:, S5:S5 + 2], CORR[:, l, c, :], ALU.add), r=["FS%d" % a, "CORR"], w=["FS%d" % a])
                    C.op("act", ACT(HL[:, c, :].rearrange("p (a b) -> p a b", a=2),
                                    PS[hb][:, W5:W5 + 64].rearrange("p (a b) -> p a b", a=2)[:, :, 0:2], AF.Copy), r=[("ps", hb)], w=["HL"])
                res.append(a)
            ag, av = res
            sg = nxt("fs", FSN)
            C.op("act", ACT(FS[sg][:, 0:N], FS[ag][:, 0:N], AF.Silu), r=["FS%d" % ag], w=["FS%d" % sg])
            C.op("pool", TT(M[:, si, 0:N], FS[sg][:, 0:N], FS[av][:, 0:N], ALU.mult), r=["FS%d" % sg, "FS%d" % av], w=[Mname])

    def ffn_down(C, t, l, grp, slots, mi, mos):
        _, lt, lc0, N = tile_info(t)
        M = BQ[mi]
        Mname = "BQM%d" % mi
        for mo in mos:
            yb = nxt("y", [4, 5, 6, 7])
            C.op("pe", [MM(PS[yb][:, 0:N], WDN[:, slots[si], mo * 128:(mo + 1) * 128], M[:, si, 0:N], start=(si == 0), stop=(si == len(grp) - 1))
                        for si in range(len(grp))], r=[("WS", sl) for sl in slots] + [Mname], w=[("ps", yb)])
            if mo % 2 == 0:
                C.op("dve", TT(XT[:, mo, lc0:lc0 + N], PS[yb][:, 0:N], XT[:, mo, lc0:lc0 + N], ALU.add), r=[("ps", yb), ("XT", lt, mo)], w=[("XT", lt, mo)])
            else:
                ya = nxt("fs", FSN)
                C.op("act", ACT(FS[ya][:, 0:N], PS[yb][:, 0:N], AF.Copy), r=[("ps", yb)], w=["FS%d" % ya])
                C.op("pool", TT(XT[:, mo, lc0:lc0 + N], FS[ya][:, 0:N], XT[:, mo, lc0:lc0 + N], ALU.add), r=["FS%d" % ya, ("XT", lt, mo)], w=[("XT", lt, mo)])

    def ffn_pass(st, l, pidx):
        tiles = [3 * st + i for i in range(3)]
        flush([lambda C: ch_rmsnorm(C, tiles[2], l * SM_L + O_G2)])
        if st == 0:
            P.op("dve", CP(CARRY[:, l, :, :], XN[:, :, 1152:1154]), r=[("XN", 2)], w=["CARRY%d" % l])
        else:
            P.op("dve", CP(XN[:, :, 0:2], CARRY[:, l, :, :]), r=["CARRY%d" % l], w=["XNc"])
        ng = len(GROUPS)
        units = [(gi, grp, t) for gi, grp in enumerate(GROUPS[:ng - 2]) for t in tiles]
        units += [(g_, GROUPS[g_], t) for t in tiles for g_ in (ng - 2, ng - 1)]

        def up(ui, only=None):
            gi, grp, t = units[ui]
            slots = [(pidx * 22 + s) % RING for s in grp]
            flush([lambda C: ffn_up(C, t, l, grp, slots, 1 + (ui % 2), only)])

        def down(ui, mos):
            gi, grp, t = units[ui]
            slots = [(pidx * 22 + s) % RING for s in grp]
            flush([lambda C: ffn_down(C, t, l, grp, slots, 1 + (ui % 2), mos)])

        MOS = [[0, 1, 2], [3, 4, 5], [6, 7]]
        up(0)
        for ui, (gi, grp, t) in enumerate(units):
            nslab = len(units[ui + 1][1]) if ui + 1 < len(units) else 0
            for part in range(3):
                if part < nslab:
                    up(ui + 1, only=[part])
                down(ui, MOS[part])
            if gi == len(GROUPS) - 1 and l == 1:
                out_y(t)
            if t == tiles[-1]:
                issue_slabs(pidx * 22 + grp[-1] + 1 + RING)
        if st == 1:
            b = nxt("y", [4, 5, 6, 7])
            stg = nxt("stg", [0, 1])
            P.op("pe", [TR(PS[b][0:44, r_ * 128:(r_ + 1) * 128], HL[:, :, r_], IDENT) for r_ in range(4)], r=["HL", "CF"], w=[("ps", b)])
            P.op("act", ACT(STG[stg][0:44, 0:512], PS[b][0:44, 0:512], AF.Copy), r=[("ps", b)], w=["STG%d" % stg])
            for r_ in range(4):
                dst = (ocp if r_ < 2 else ocs)[l, r_ % 2, :].rearrange("(c p) -> c p", p=128)
                P.dma("sp", dst, STG[stg][0:44, r_ * 128:(r_ + 1) * 128], r=["STG%d" % stg])

    for l in range(2):
        stg = nxt("stg", [0, 1])
        P.dma("sp", STG[stg][:, 0:128], ck[l, :, :], w=["STG%d" % stg])
        b = nxt("proj", [0, 1, 2])
        P.op("pe", TR(PS[b][:, 0:128], STG[stg][:, 0:128], IDENT), r=["STG%d" % stg, "CF"], w=[("ps", b)])
        P.op("act", ACT(CKT[:, l, :], PS[b][:, 0:128], AF.Copy), r=[("ps", b)], w=["CKT"])

    load_mixer_weights(0, gate=True)
    pidx = 0
    for st in range(2):
        for l in range(2):
            if st == 1:
                P.op("dve", CP(KT[:, 1024:1152], KC[:, l, :]), r=["KC"], w=["KT"])
                P.op("dve", CP(VB[:, 8, :], VC[:, l, :]), r=["VC"], w=[("VB", 8)])
            tl = list(range(3 * st, 3 * st + 3))
            if l == 0 and st == 0:
                load_x(1, after=["WIN"])
                load_x(2, after=["WIN"])
            for i, t in enumerate(tl):
                if st == 1 and l == 0 and i == 1:
                    P.op("dve", MS(XT[:, :, LG:LG + 16], 0.0), w=[("XT", 2, kc) for kc in range(8)])
                if i < 2:
                    pend = mixer_tile(t, l, prologue_done=(i > 0), next_t=tl[i + 1], defer_wout=True, pre_wouts=(pend if i > 0 else ()))
                else:
                    extra = [(lambda C, t_=t_: ch_rmsnorm(C, t_, l * SM_L + O_G2)) for t_ in tl[0:2]]
                    mixer_tile(t, l, prologue_done=True, next_t=None, extra_tail=extra, pre_wouts=pend)
                if pidx == 0 and i == 0:
                    issue_slabs(RING)
            if st == 0:
                P.op("dve", CP(KC[:, l, :], KT[:, 1024:1152]), r=["KT"], w=["KC"])
                P.op("dve", CP(VC[:, l, :], VB[:, 8, :]), r=[("VB", 8)], w=["VC"])
            if pidx + 1 < 4:
                load_mixer_weights((pidx + 1) % 2)
            ffn_pass(st, l, pidx)
            pidx += 1
    P.finish()
    return nc


def _prep(inputs):
    f = lambda a: np.ascontiguousarray(np.asarray(a, dtype=np.float32))
    x_prompt = f(inputs["x_prompt"]); x_sample = f(inputs["x_sample"])
    cache_k = f(inputs["cache_k"]); cache_v = f(inputs["cache_v"])
    state_pool = f(inputs["state_pool"]); state_conv = f(inputs["state_conv"])
    w_in = f(inputs["w_in"]); w_out = f(inputs["w_out"]); w_pool = f(inputs["w_pool"])
    w_up = f(inputs["w_up"]); w_down = f(inputs["w_down"])
    qperm = np.concatenate([np.concatenate([np.arange(j * 64, j * 64 + 64), np.arange((4 + j) * 64, (4 + j) * 64 + 64)]) for j in range(4)])
    colperm = np.concatenate([qperm, np.arange(512, 640), np.arange(768, 1280), np.arange(640, 768)])
    win = np.ascontiguousarray(w_in[:, :, colperm].reshape(2, 8, 128, 1280).transpose(0, 2, 1, 3))
    rowperm = np.concatenate([qperm, np.arange(512, 1024)])
    wout = np.ascontiguousarray(w_out[:, rowperm, :].reshape(2, 8, 128, 1024).transpose(0, 2, 1, 3))
    wpool = np.ascontiguousarray(w_pool.transpose(0, 2, 1, 3))
    wu = w_up.reshape(2, 8, 128, 2, 22, 128)
    wup = np.ascontiguousarray(wu.transpose(0, 4, 2, 1, 3, 5).reshape(2, 22, 128, 8, 256))
    wdn = np.ascontiguousarray(w_down.reshape(2, 22, 128, 1024))
    small = np.zeros((128, 405), np.float32)
    for l in range(2):
        o = l * SM_L
        small[:, o + O_G1:o + O_G1 + 8] = f(inputs["norm_mix"])[l].reshape(8, 128).T
        small[:, o + O_G2:o + O_G2 + 8] = f(inputs["norm_ffn"])[l].reshape(8, 128).T
        small[:, o + O_GQ] = np.tile(f(inputs["q_norm"])[l], 2)
        small[:, o + O_GK] = np.tile(f(inputs["k_norm"])[l], 2)
        sk = f(inputs["attn_sinks"])[l].reshape(2, 4)
        small[:, o + O_SINK:o + O_SINK + 4] = np.repeat(sk, 64, axis=0)
        small[:, o + O_PSC:o + O_PSC + 4] = f(inputs["pool_scale"])[l].reshape(4, 128).T
        small[:, o + O_CW:o + O_CW + 132] = f(inputs["conv_w"])[l].reshape(3, 44, 128).transpose(2, 0, 1).reshape(128, 132)
        small[:, o + O_CB:o + O_CB + 44] = f(inputs["conv_b"])[l].reshape(44, 128).T
    small[:, O_EPS] = EPS
    constf = np.zeros((128, 192), np.float32)
    constf[:, 0:128] = np.eye(128, dtype=np.float32)
    for g in range(4):
        wv = 2 << g
        constf[:, 128 + g * 16:128 + (g + 1) * 16] = 1.0 / np.minimum(np.arange(16) + 1, wv).astype(np.float32)
    constb = np.zeros((128, 448), np.float32)
    constb[:, 0:128] = 1.0 / 1024.0
    constb[0:64, 128:192] = 1.0 / 64.0
    constb[64:128, 192:256] = 1.0 / 64.0
    d = np.arange(128) % 64
    for m in range(128):
        if d[m] < 8:
            constb[m + 8, 256 + m] = 1.0
        elif d[m] < 16:
            constb[m - 8, 256 + m] = 1.0
    constb[:, 384:448] = 1.0
    inv = (500000.0 ** (-np.arange(0, 16, 2, dtype=np.float32) / 16.0)).astype(np.float32)
    in_maps = []
    for c in range(NCORES):
        s, hf = divmod(c, 2)
        t0 = 0 if hf == 0 else 4096 - TCORE
        pos = np.zeros((NTILE, NX), np.float32)
        for t in range(NTILE):
            pos[t, 0:W] = t0 + t * W + np.arange(W)
        pos[5, S5:S5 + 16] = 4096 + np.arange(16)
        ang = pos[None, :, :] * inv[:, None, None]
        cosv, sinv = np.cos(ang).astype(np.float32), np.sin(ang).astype(np.float32)
        rope = np.zeros((128, 2, NTILE, NX), np.float32)
        rope[:, 0] = 1.0
        for p in range(128):
            dd = p % 64
            if dd < 8:
                rope[p, 0] = cosv[dd]; rope[p, 1] = -sinv[dd]
            elif dd < 16:
                rope[p, 0] = cosv[dd - 8]; rope[p, 1] = sinv[dd - 8]
        in_maps.append({
            "xp": np.ascontiguousarray(x_prompt[s, t0:t0 + TCORE, :].T).reshape(8, 128, TCORE),
            "xs": np.ascontiguousarray(x_sample[c].T).reshape(8, 128, 16),
            "ck": np.ascontiguousarray(cache_k[:, c].reshape(2, 128, 128)),
            "cv": np.ascontiguousarray(cache_v[:, c].reshape(2, 128, 128)),
            "spool": np.ascontiguousarray(state_pool[:, c]),
            "sconv": np.ascontiguousarray(state_conv[:, c]),
            "win": win, "wout": wout, "wpool": wpool, "wup": wup, "wdn": wdn,
            "small": small, "constf": constf, "constb": constb, "rope": rope,
        })
    return in_maps


_NC_CACHE = {}


def kernel(**inputs):
    in_maps = _prep(inputs)
    if "nc" not in _NC_CACHE:
        _NC_CACHE["nc"] = build_nc()
    nc = _NC_CACHE["nc"]
    res = run_bass_kernel_spmd(nc, in_maps, core_ids=list(range(NCORES)))
    R = res.results
    y_prompt = np.zeros((4, 4096, 1024), np.float32)
    for s in range(4):
        y_prompt[s, 0:TCORE] = R[2 * s]["yp"].reshape(1024, TCORE).T
        y_prompt[s, TCORE:4096] = R[2 * s + 1]["yp"].reshape(1024, TCORE).T[2 * TCORE - 4096:]
    y_sample = np.stack([R[c]["ys"].reshape(1024, 16).T for c in range(8)])
    pk = lambda name, shape: np.stack([np.stack([R[2 * s + 1][name][l] for s in range(4)]) for l in range(2)]).reshape(shape)
    sk = lambda name, shape: np.stack([np.stack([R[c][name][l] for c in range(8)]) for l in range(2)]).reshape(shape)
    return (y_prompt, y_sample,
            pk("okp", (2, 4, 128, 2, 64)), pk("ovp", (2, 4, 128, 2, 64)), pk("opp", (2, 4, 15, 512)), pk("ocp", (2, 4, 2, 5632)),
            sk("oks", (2, 8, 128, 2, 64)), sk("ovs", (2, 8, 128, 2, 64)), sk("ops", (2, 8, 15, 512)), sk("ocs", (2, 8, 2, 5632)))
```

```python
import os
import numpy as np
import concourse.bass as bass
import concourse.mybir as mybir
from concourse.bass_utils import run_bass_kernel_spmd

F32 = mybir.dt.float32
BF16 = mybir.dt.bfloat16
AF = mybir.ActivationFunctionType
ALU = mybir.AluOpType

NCORES = 8
TCORE = 2176
W = 384
NTILE = 6
LW = 1184
NX = 416
W5 = 256
S5 = W5 + 16
N5 = W5 + 32
LG = 2 * W + W5
LS = LG + 16
G = 3
RING = 2 * G
GROUPS = [list(range(i, min(i + G, 22))) for i in range(0, 22, G)]
EPS = 1e-6
SM_L = 202
O_G1, O_G2, O_GQ, O_GK, O_SINK, O_PSC, O_CW, O_CB = 0, 8, 16, 17, 18, 22, 26, 158
O_EPS = 404


class Prog:
    def __init__(self, nc):
        self.nc = nc
        self.names = ["pe", "act", "dve", "pool", "sp"]
        self.eng = {"pe": nc.tensor, "act": nc.scalar, "dve": nc.vector, "pool": nc.gpsimd, "sp": nc.sync}
        self.sem = {n: nc.alloc_semaphore("s_" + n) for n in self.names}
        self.cnt = {n: 0 for n in self.names}
        self.lists = {n: [] for n in self.names}
        self.seen = {n: {} for n in self.names}
        self.lastw = {}
        self.readers = {}
        self.ndsem = 56
        self.dsem = [nc.alloc_semaphore("d%d" % i) for i in range(self.ndsem)]
        self.dval = [0] * self.ndsem
        self.dnext = 0
        self.alias = {}

    def _wait(self, eng, tok):
        if tok is None:
            return
        kind, key, val = tok
        if kind == "e" and key == eng and eng in ("pe", "sp"):
            return
        k = (kind, key)
        if self.seen[eng].get(k, 0) >= val:
            return
        self.seen[eng][k] = val
        sem = self.sem[key] if kind == "e" else self.dsem[key]
        self.lists[eng].append(lambda e, sem=sem, val=val: e.wait_ge(sem, val))

    def _canon(self, regs):
        out = []
        for reg in regs:
            out.extend(self.alias.get(reg, [reg]))
        return out

    def _deps(self, eng, r, w, is_dma=False):
        r, w = self._canon(r), self._canon(w)
        for reg in r:
            for tok in self.lastw.get(reg, ()):
                self._wait(eng, tok)
        for reg in w:
            for tok in self.lastw.get(reg, ()):
                if is_dma and tok[0] == "d" and not self.readers.get(reg):
                    continue
                self._wait(eng, tok)
            for tok in self.readers.get(reg, ()):
                self._wait(eng, tok)

    def _commit(self, tok, r, w):
        r, w = self._canon(r), self._canon(w)
        for reg in r:
            self.readers.setdefault(reg, []).append(tok)
        for reg in w:
            prev = self.lastw.get(reg, [])
            if tok[0] == "d" and prev and all(p[0] == "d" for p in prev) and not self.readers.get(reg):
                self.lastw[reg] = prev + [tok]
            else:
                self.lastw[reg] = [tok]
            self.readers[reg] = []

    def op(self, eng, insts, r=(), w=()):
        if isinstance(insts, tuple):
            insts = [insts]
        self._deps(eng, r, w)
        self.cnt[eng] += 1
        sem = self.sem[eng]

        def run(e, insts=insts, sem=sem):
            ins = None
            for meth, kw in insts:
                ins = getattr(e, meth)(**kw)
            ins.then_inc(sem, 1)
        self.lists[eng].append(run)
        self._commit(("e", eng, self.cnt[eng]), r, w)

    def dma(self, q, out, in_, r=(), w=(), **kw):
        self._deps(q, r, w, is_dma=True)
        i = self.dnext
        self.dnext = (self.dnext + 1) % self.ndsem
        if self.dval[i] > 0:
            self._wait(q, ("d", i, self.dval[i]))
        self.dval[i] += 16
        sem = self.dsem[i]
        self.lists[q].append(lambda e, out=out, in_=in_, sem=sem, kw=kw: e.dma_start(out=out, in_=in_, **kw).then_inc(sem, 16))
        self._commit(("d", i, self.dval[i]), r, w)

    def finish(self):
        for i in range(self.ndsem):
            if self.dval[i] > 0:
                self._wait("sp", ("d", i, self.dval[i]))
        for n in self.names:
            if n != "sp" and self.cnt[n] > 0:
                self._wait("sp", ("e", n, self.cnt[n]))
        with self.nc.Block() as block:
            def run(name):
                def f(e):
                    for fn in self.lists[name]:
                        fn(e)
                return f
            block.sync(run("sp"))
            block.tensor(run("pe"))
            block.scalar(run("act"))
            block.vector(run("dve"))
            block.gpsimd(run("pool"))


def build_nc():
    nc = bass.Bass("TRN2", target_bir_lowering=False)
    P = Prog(nc)

    def din(name, shape):
        return nc.dram_tensor(name, list(shape), F32, kind="ExternalInput").ap()

    def dout(name, shape):
        return nc.dram_tensor(name, list(shape), F32, kind="ExternalOutput").ap()

    xp = din("xp", [8, 128, TCORE]); xs = din("xs", [8, 128, 16])
    ck = din("ck", [2, 128, 128]); cv = din("cv", [2, 128, 128])
    spool = din("spool", [2, 15, 512]); sconv = din("sconv", [2, 2, 5632])
    win = din("win", [2, 128, 8, 1280]); wout = din("wout", [2, 128, 8, 1024]); wpool = din("wpool", [2, 128, 4, 128])
    wup = din("wup", [2, 22, 128, 8, 256]); wdn = din("wdn", [2, 22, 128, 1024])
    small = din("small", [128, 405]); constf = din("constf", [128, 192]); constb = din("constb", [128, 448])
    rope = din("rope", [128, 2, NTILE, NX])
    yp = dout("yp", [8, 128, TCORE]); ys = dout("ys", [8, 128, 16])
    okp = dout("okp", [2, 128, 128]); ovp = dout("ovp", [2, 128, 128]); opp = dout("opp", [2, 15, 512]); ocp = dout("ocp", [2, 2, 5632])
    oks = dout("oks", [2, 128, 128]); ovs = dout("ovs", [2, 128, 128]); ops_ = dout("ops", [2, 15, 512]); ocs = dout("ocs", [2, 2, 5632])

    sb = nc.alloc_sbuf_tensor
    XT = sb("XT", [128, 8, LW], F32)
    XN = sb("XN", [128, 8, LW + 2], BF16)
    KT = sb("KT", [128, 2320], BF16)
    VB = sb("VB", [128, 19, 128], BF16)
    KC = sb("KC", [128, 2, 128], BF16); VC = sb("VC", [128, 2, 128], BF16)
    WIN = sb("WIN", [128, 8, 1280], BF16); WOUT = sb("WOUT", [128, 8, 1024], BF16); WPOOL = sb("WPOOL", [128, 4, 128], BF16)
    WUP = sb("WUP", [128, RING, 8, 256], BF16); WDN = sb("WDN", [128, RING, 1024], BF16)
    SMALL = sb("SMALL", [128, 405], F32); CF = sb("CF", [128, 192], F32); CB = sb("CB", [128, 448], BF16)
    ROPE = sb("ROPE", [128, 2, NX], F32)
    U = sb("U", [128, 4, 15 + NX], F32)
    FS = [sb("FS%d" % i, [128, 432], F32) for i in range(8)]
    BQ = [sb("BQ%d" % i, [128, 4, NX], BF16) for i in range(4)]
    BS = [sb("BS%d" % i, [128, NX], BF16) for i in range(6)]
    PAB = sb("PAB", [128, 2, 2, 512], BF16)
    RD = sb("RD", [128, 512], F32)
    STG = [sb("STG%d" % i, [128, 1024], F32) for i in range(2)]
    ESINK = sb("ESINK", [128, 8], F32)
    UC = sb("UC", [128, 2, 4, 15], F32); CARRY = sb("CARRY", [128, 2, 8, 2], BF16)
    SPT = sb("SPT", [128, 2, 4, 15], F32); SCT = sb("SCT", [128, 2, 2, 44], F32); CORR = sb("CORR", [128, 2, 44, 2], F32)
    HL = sb("HL", [128, 44, 4], F32); TMPC = sb("TMPC", [128, 44], F32)
    CKT = sb("CKT", [128, 2, 128], BF16); CV = sb("CV", [128, 2, 128], BF16)
    PS = [nc.alloc_psum_tensor("PS%d" % i, [128, 512], F32) for i in range(8)]

    IDENT = CF[:, 0:128]
    INVC = CF[:, 128:192]
    ONES_D = CB[:, 0:128]; BLK = CB[:, 128:256]; PERM = CB[:, 256:384]; ONESK = CB[:, 384:448]
    EPSC = SMALL[:, O_EPS:O_EPS + 1]

    def sm(l, off, n=1):
        return SMALL[:, l * SM_L + off: l * SM_L + off + n]

    rot = {}

    def nxt(key, choices):
        v = choices[rot.get(key, 0) % len(choices)]
        rot[key] = rot.get(key, 0) + 1
        return v

    def MM(out, lhsT, rhs, start=True, stop=True, tp=None):
        kw = dict(out=out, lhsT=lhsT, rhs=rhs, start=start, stop=stop)
        if tp is not None:
            kw["tile_position"] = tp
        return ("matmul", kw)

    def TR(out, in_, ident):
        return ("transpose", dict(out=out, in_=in_, identity=ident))

    def ACT(out, in_, func, bias=None, scale=None):
        kw = dict(out=out, in_=in_, func=func)
        if bias is not None:
            kw["bias"] = bias
        if scale is not None:
            kw["scale"] = scale
        return ("activation", kw)

    def TT(out, in0, in1, op):
        return ("tensor_tensor", dict(out=out, in0=in0, in1=in1, op=op))

    def STT(out, in0, scalar, in1, op0, op1):
        return ("scalar_tensor_tensor", dict(out=out, in0=in0, scalar=scalar, in1=in1, op0=op0, op1=op1))

    def CP(out, in_):
        return ("tensor_copy", dict(out=out, in_=in_))

    def MS(ap, v):
        return ("memset", dict(ap=ap, constant=v))

    def RCP(out, in_):
        return ("reciprocal", dict(out=out, in_=in_))

    def v4(ap):
        return ap.rearrange("p (a b) -> p a b", a=4)

    FSN = list(range(8))

    P.dma("sp", SMALL[:], small[:, :], w=["SMALL"])
    for h in range(4):
        P.dma("act", XT[:, 2 * h:2 * h + 2, 0:W], xp[2 * h:2 * h + 2, :, 0:W].rearrange("k p t -> p k t"),
              w=[("XT", 0, kc) for kc in range(2 * h, 2 * h + 2)])
    P.dma("sp", CF[:], constf[:, :], w=["CF"])
    P.dma("pool", CB[:], constb[:, :], w=["CB"])
    for l in range(2):
        P.dma("pool", CV[:, l, :], cv[l, :, :], w=["CV"])
    P.op("dve", MS(PAB[:], 0.0), w=["PA0", "PA1", "PB0", "PB1"])
    P.op("dve", MS(XN[:, :, 0:2], 0.0), w=["XNc"])
    P.op("dve", MS(UC[:], 0.0), w=[("UC", l, g) for l in range(2) for g in range(4)])
    P.alias[("ps", 6)] = [("ps", 6, 0), ("ps", 6, 1)]
    P.alias[("ps", 7)] = [("ps", 7, 0), ("ps", 7, 1)]
    P.alias["BQM1"] = ["AT"]
    P.alias["BQM2"] = [("D", g) for g in range(4)]
    bq_regions = [[("QT", j) for j in range(4)], ["AT"], [("D", g) for g in range(4)], [("PL", g) for g in range(4)]]
    for i in range(4):
        P.op("dve", MS(BQ[i][:], 0.0), w=bq_regions[i])
    P.op("dve", MS(U[:], 0.0), w=[("U", g) for g in range(4)] + [("Uc", g) for g in range(4)])
    P.op("dve", MS(HL[:], 0.0), w=["HL"])
    for l in range(2):
        P.op("act", ACT(ESINK[:, 4 * l:4 * l + 4], sm(l, O_SINK, 4), AF.Exp), r=["SMALL"], w=["ESINK"])
    for l in range(2):
        stg = nxt("stg", [0, 1])
        P.dma("sp", STG[stg][0:15, 0:512], spool[l, :, :], w=["STG%d" % stg])
        P.op("pe", [TR(PS[0][:, g * 16:g * 16 + 15], STG[stg][0:15, g * 128:(g + 1) * 128], CF[0:15, 0:15]) for g in range(4)],
             r=["STG%d" % stg, "CF"], w=[("ps", 0)])
        P.op("act", ACT(SPT[:, l, :, :], PS[0][:, 0:64].rearrange("p (a b) -> p a b", a=4)[:, :, 0:15], AF.Copy), r=[("ps", 0)], w=["SPT"])
        stg = nxt("stg", [0, 1])
        P.dma("sp", STG[stg][0:88, 0:128], sconv[l, :, :].rearrange("r (c p) -> (r c) p", p=128), w=["STG%d" % stg])
        P.op("pe", TR(PS[1][:, 0:88], STG[stg][0:88, 0:128], CF[0:88, 0:88]), r=["STG%d" % stg, "CF"], w=[("ps", 1)])
        P.op("act", ACT(SCT[:, l, :, :].rearrange("p a b -> p (a b)"), PS[1][:, 0:88], AF.Copy), r=[("ps", 1)], w=["SCT"])
    for l in range(2):
        cw0 = sm(l, O_CW, 44)
        cw1 = sm(l, O_CW + 44, 44)
        P.op("dve", TT(CORR[:, l, :, 1], cw0, SCT[:, l, 1, :], ALU.mult), r=["SMALL", "SCT"], w=["CORR"])
        P.op("dve", TT(TMPC[:], cw1, SCT[:, l, 1, :], ALU.mult), r=["SMALL", "SCT"], w=["TMPC"])
        P.op("dve", TT(CORR[:, l, :, 0], cw0, SCT[:, l, 0, :], ALU.mult), r=["SMALL", "SCT"], w=["CORR"])
        P.op("dve", TT(CORR[:, l, :, 0], CORR[:, l, :, 0], TMPC[:], ALU.add), r=["TMPC", "CORR"], w=["CORR"])

    def load_mixer_weights(l, gate=False):
        after = ["WIN"] if gate else []
        for h in range(4):
            P.dma("pool", WIN[:, 2 * h:2 * h + 2, :], win[l, :, 2 * h:2 * h + 2, :], w=["WIN"])
        for h in range(2):
            P.dma("pool", WOUT[:, 4 * h:4 * h + 4, :], wout[l, :, 4 * h:4 * h + 4, :], r=after, w=["WOUT"])
        P.dma("pool", WPOOL[:], wpool[l, :, :, :], r=after, w=["WPOOL"])

    slab_q = {"issued": 0}
    passes = [(st, l) for st in range(2) for l in range(2)]

    def issue_slabs(upto, after=()):
        while slab_q["issued"] < min(upto, len(passes) * 22):
            q = slab_q["issued"]
            l = passes[q // 22][1]
            s = q % 22
            slot = q % RING
            P.dma("pool", WUP[:, slot, :, :], wup[l, s, :, :, :], r=list(after), w=[("WS", slot)])
            P.dma("pool", WDN[:, slot, :], wdn[l, s, :, :], r=list(after), w=[("WS", slot)])
            slab_q["issued"] += 1

    def tile_info(t):
        st, lt = divmod(t, 3)
        N = N5 if t == 5 else W
        return st, lt, lt * W, N

    def nsub(t):
        return 2 if t == 5 else 3

    class NoRes(Exception):
        pass

    free_lists = {"st": [3, 4, 6, 7], "proj": [0, 1, 2], "nb": [5], "bs": list(range(len(BS))), "fs": list(range(8)), "stg": [0, 1]}

    class Chain:
        def __init__(self):
            self.items = []
            self.owned = []

        def op(self, eng, insts, r=(), w=()):
            self.items.append(("op", (eng, insts), dict(r=list(r), w=list(w))))

        def dma(self, q, out, in_, r=(), w=(), **kw):
            self.items.append(("dma", (q, out, in_), dict(r=list(r), w=list(w), **kw)))

        def alloc(self, kind):
            fl = free_lists[kind]
            if not fl:
                raise NoRes(kind)
            x = fl.pop(0)
            self.owned.append((kind, x))
            return x

        def free(self, kind, x):
            self.items.append(("free", (kind, x), {}))

        def peek(self, kind):
            fl = free_lists[kind]
            if not fl:
                raise NoRes(kind)
            return fl[0]

        def hold(self):
            self.items.append(("hold", (), {}))

        def unhold(self):
            self.items.append(("unhold", (), {}))

    sim = {"eng": {n: 0.0 for n in P.names}, "now": 0.0, "reg": {}}

    def regs_ready(kw):
        t_ = 0.0
        for reg in P._canon(list(kw.get("r", ())) + list(kw.get("w", ()))):
            t_ = max(t_, sim["reg"].get(reg, 0.0))
        return t_

    def op_cost(kind, a):
        if kind == "dma":
            return 60.0, 2000.0
        eng, insts = a
        if isinstance(insts, tuple):
            insts = [insts]
        tot = 0.0
        for meth, kw in insts:
            o = kw.get("out", kw.get("ap"))
            n = 1
            for d_ in o.shape[1:]:
                n *= d_
            if eng == "pe":
                tot += 20 + n / (1.2 if "tile_position" in kw else 1.9)
            elif eng == "act":
                tot += 150 + 0.8 * n
            elif eng == "dve":
                tot += 80 + 1.15 * n
            else:
                tot += 50 + 2.1 * n
        return tot, tot + 100.0

    def flush(factories, width=3):
        if os.environ.get("K_NOILV") is not None:
            width = 1
        pending = list(factories)
        active = []
        pos = {}
        ready = {}

        def release(c, kind, x):
            c.owned.remove((kind, x))
            free_lists[kind].append(x)

        def try_start():
            while pending and len(active) < width:
                c = Chain()
                try:
                    pending[0](c)
                except NoRes:
                    for kind, x in c.owned:
                        free_lists[kind].insert(0, x)
                    if not active:
                        raise
                    return
                pending.pop(0)
                if c.items:
                    active.append(c)
                    pos[id(c)] = 0
                    ready[id(c)] = sim["now"]

        def head(c):
            kind, a, kw = c.items[pos[id(c)]]
            return kind, a, kw

        def emit_one(c):
            kind, a, kw = head(c)
            eng = a[0]
            busy, lat = op_cost(kind, a)
            start = max(sim["eng"][eng], ready[id(c)], regs_ready(kw))
            (P.op if kind == "op" else P.dma)(*a, **kw)
            sim["eng"][eng] = start + busy
            ready[id(c)] = start + lat
            for reg in P._canon(list(kw.get("w", ()))):
                sim["reg"][reg] = start + lat
            sim["now"] = max(sim["now"], start)
            pos[id(c)] += 1
            return True

        def skip_markers(c, held):
            while pos[id(c)] < len(c.items):
                kind, a, kw = c.items[pos[id(c)]]
                if kind == "free":
                    release(c, *a)
                elif kind == "hold":
                    held = True
                elif kind == "unhold":
                    held = False
                else:
                    return True, held
                pos[id(c)] += 1
            active.remove(c)
            for kind2, x in list(c.owned):
                release(c, kind2, x)
            return False, False

        def start_chains():
            n0 = len(active)
            try_start()
            for c in list(active[n0:]):
                skip_markers(c, False)

        start_chains()
        while active:
            best, best_t = None, None
            for c in active:
                kind, a, kw = head(c)
                t_ = max(sim["eng"][a[0]], ready[id(c)], regs_ready(kw))
                if best is None or t_ < best_t:
                    best, best_t = c, t_
            c = best
            held = False
            while True:
                emit_one(c)
                alive, held = skip_markers(c, held)
                if not alive or not held:
                    break
            start_chains()

    def load_x(t, queues=("sp",), after=()):
        if True:
            _, lt, lc0, N = tile_info(t)
            wt = nsub(t) * 128
            for h in range(4):
                P.dma(queues[h % len(queues)], XT[:, 2 * h:2 * h + 2, lc0:lc0 + wt], xp[2 * h:2 * h + 2, :, t * W:t * W + wt].rearrange("k p t -> p k t"),
                      r=list(after), w=[("XT", lt, kc) for kc in range(2 * h, 2 * h + 2)])
            if t == 5:
                P.dma("sp", XT[:, :, LS:LS + 16], xs[:, :, :].rearrange("k p t -> p k t"), w=[("XT", 2, kc) for kc in range(8)])

    def ch_rmsnorm(C, t, gcol):
        st, lt, lc0, N = tile_info(t)
        nb = C.alloc("nb")
        sqs = [C.alloc("bs"), C.alloc("bs")]
        fs = C.alloc("fs")
        for kc in range(8):
            sq = sqs[kc % 2]
            C.op("act", ACT(BS[sq][:, 0:N], XT[:, kc, lc0:lc0 + N], AF.Square), r=[("XT", lt, kc)], w=["BS%d" % sq])
            C.op("pe", MM(PS[nb][:, 0:N], ONES_D, BS[sq][:, 0:N], start=(kc == 0), stop=(kc == 7)), r=["BS%d" % sq, "CB"], w=[("ps", nb)])
        C.free("bs", sqs[0])
        C.free("bs", sqs[1])
        C.op("act", ACT(FS[fs][:, 0:N], PS[nb][:, 0:N], AF.Ln, bias=EPSC, scale=1.0), r=[("ps", nb), "SMALL"], w=["FS%d" % fs])
        C.op("act", ACT(FS[fs][:, 0:N], FS[fs][:, 0:N], AF.Exp, scale=-0.5), r=["FS%d" % fs], w=["FS%d" % fs])
        for kc in range(8):
            C.op("dve", STT(XN[:, kc, 2 + lc0:2 + lc0 + N], XT[:, kc, lc0:lc0 + N], SMALL[:, gcol + kc:gcol + kc + 1], FS[fs][:, 0:N], ALU.mult, ALU.mult),
                 r=[("XT", lt, kc), "FS%d" % fs, "SMALL"], w=[("XN", lt)])
        if t == 5:
            C.op("dve", MS(XN[:, :, 2 + LG:2 + LS], 0.0), w=[("XN", 2)])

    def proj_op(C, t, col0, b):
        st, lt, lc0, N = tile_info(t)
        C.op("pe", [MM(PS[b][:, 0:N], WIN[:, kc, col0:col0 + 128], XN[:, kc, 2 + lc0:2 + lc0 + N], start=(kc == 0), stop=(kc == 7)) for kc in range(8)],
             r=["WIN", ("XN", lt)], w=[("ps", b)])

    def ch_qk(C, t, l, j):
        st, lt, lc0, N = tile_info(t)
        b = C.alloc("proj")
        sq = C.alloc("bs")
        qg = C.alloc("bs")
        fr = C.alloc("fs")
        f1 = C.alloc("fs")
        f2 = C.alloc("fs")
        mb = C.alloc("st")
        wb = C.alloc("st")
        if j == 4 and t == 5:
            b2 = C.alloc("proj")
            stgs = [C.alloc("stg"), C.alloc("stg")]
        proj_op(C, t, j * 128, b)
        gcol = sm(l, O_GQ if j < 4 else O_GK)
        C.op("act", ACT(BS[sq][:, 0:N], PS[b][:, 0:N], AF.Square), r=[("ps", b)], w=["BS%d" % sq])
        C.op("act", ACT(BS[qg][:, 0:N], PS[b][:, 0:N], AF.Identity, scale=gcol), r=[("ps", b), "SMALL"], w=["BS%d" % qg])
        C.free("proj", b)
        C.op("pe", MM(PS[mb][:, 0:N], BLK, BS[sq][:, 0:N]), r=["BS%d" % sq, "CB"], w=[("ps", mb)])
        C.free("bs", sq)
        C.op("pe", MM(PS[wb][:, 0:N], PERM, BS[qg][:, 0:N]), r=["BS%d" % qg, "CB"], w=[("ps", wb)])
        C.op("act", ACT(FS[fr][:, 0:N], PS[mb][:, 0:N], AF.Ln, bias=EPSC, scale=1.0), r=[("ps", mb), "SMALL"], w=["FS%d" % fr])
        C.op("act", ACT(FS[fr][:, 0:N], FS[fr][:, 0:N], AF.Exp, scale=-0.5), r=["FS%d" % fr], w=["FS%d" % fr])
        C.free("st", mb)
        C.op("dve", TT(FS[f1][:, 0:N], PS[wb][:, 0:N], ROPE[:, 1, 0:N], ALU.mult), r=[("ps", wb), "ROPE"], w=["FS%d" % f1])
        C.free("st", wb)
        C.op("dve", TT(FS[f2][:, 0:N], BS[qg][:, 0:N], ROPE[:, 0, 0:N], ALU.mult), r=["BS%d" % qg, "ROPE"], w=["FS%d" % f2])
        C.free("bs", qg)
        C.op("pool", TT(FS[f1][:, 0:N], FS[f1][:, 0:N], FS[f2][:, 0:N], ALU.add), r=["FS%d" % f1, "FS%d" % f2], w=["FS%d" % f1])
        if j < 4:
            C.op("dve", TT(BQ[0][:, j, 0:N], FS[f1][:, 0:N], FS[fr][:, 0:N], ALU.mult), r=["FS%d" % f1, "FS%d" % fr], w=[("QT", j)])
            return
        C.op("dve", TT(FS[f1][:, 0:N], FS[f1][:, 0:N], FS[fr][:, 0:N], ALU.mult), r=["FS%d" % f1, "FS%d" % fr], w=["FS%d" % f1])
        cc0 = t * W
        wk = W5 if t == 5 else W
        C.op("act", ACT(KT[:, cc0:cc0 + wk], FS[f1][:, 0:wk], AF.Copy), r=["FS%d" % f1], w=["KT"])
        if t == 5:
            C.op("act", ACT(KT[:, 2304:2320], FS[f1][:, S5:S5 + 16], AF.Copy), r=["FS%d" % f1], w=["KT"])
            for i_, (src, nrows, dst) in enumerate(((FS[f1][:, W5 - 128:W5], 128, okp[l, :, :]), (FS[f1][:, S5:S5 + 16], 16, oks[l, 112:128, :]))):
                stg = stgs[i_]
                C.op("pe", TR(PS[b2][0:nrows, 0:128], src, IDENT), r=["FS%d" % f1, "CF"], w=[("ps", b2)])
                C.op("act", ACT(STG[stg][0:nrows, 0:128], PS[b2][0:nrows, 0:128], AF.Copy), r=[("ps", b2)], w=["STG%d" % stg])
                C.dma("sp", dst, STG[stg][0:nrows, 0:128], r=["STG%d" % stg])
            C.dma("sp", oks[l, 0:112, :], ck[l, 16:128, :])

    def ch_u(C, t, l, g):
        st, lt, lc0, N = tile_info(t)
        b = C.alloc("proj")
        C.op("dve", CP(U[:, g, 0:15], UC[:, l, g, :]), r=[("UC", l, g)], w=[("Uc", g)])
        proj_op(C, t, 640 + g * 128, b)
        C.op("act", ACT(U[:, g, 15:15 + N], PS[b][:, 0:N], AF.Copy), r=[("ps", b)], w=[("U", g)])
        if t == 5:
            C.op("dve", CP(U[:, g, S5:S5 + 15], SPT[:, l, g, :]), r=["SPT"], w=[("U", g)])
        C.op("dve", CP(UC[:, l, g, :], U[:, g, W:W + 15]), r=[("U", g)], w=[("UC", l, g)])

    def ch_v(C, t, l, sub):
        st, lt, lc0, N = tile_info(t)
        ns = nsub(t)
        m, c0, slot = (128, lc0 + sub * 128, t * 3 + sub) if sub < ns else (16, LS, 18)
        b = C.alloc("proj")
        if t == 5 and sub >= ns - 1:
            stg = C.alloc("stg")
        C.op("pe", [MM(PS[b][0:m, 0:128], XN[:, kc, 2 + c0:2 + c0 + m], WIN[:, kc, 1152:1280], start=(kc == 0), stop=(kc == 7)) for kc in range(8)],
             r=["WIN", ("XN", lt)], w=[("ps", b)])
        C.op("act", ACT(VB[0:m, slot, :], PS[b][0:m, 0:128], AF.Copy), r=[("ps", b)], w=[("VB", slot)])
        if t == 5 and sub >= ns - 1:
            C.op("act", ACT(STG[stg][0:m, 0:128], PS[b][0:m, 0:128], AF.Copy), r=[("ps", b)], w=["STG%d" % stg])
            if sub == ns - 1:
                C.dma("sp", ovp[l, :, :], STG[stg][0:128, 0:128], r=["STG%d" % stg])
            else:
                C.dma("sp", ovs[l, 112:128, :], STG[stg][0:16, 0:128], r=["STG%d" % stg])
                C.dma("sp", ovs[l, 0:112, :], cv[l, 16:128, :])

    QTALL = [("QT", j) for j in range(4)]

    def attn_unit(t, l, pi, g, sbank):
        QT = BQ[0]
        Pg = t * 3 + pi
        q0 = pi * 128
        hasA = Pg > 0
        pr = slice(g * 64, (g + 1) * 64)
        qrhs = QT[pr, :, q0:q0 + 128]
        tpk = (g * 64, 0)
        tpo = (0, g * 64)
        pa = v4(PAB[:, g, 0, :])
        pb = v4(PAB[:, g, 1, :])
        bA, bB = sbank
        sa = v4(PS[bA][:, :])
        sB = v4(PS[bB][:, :])
        vA = [("VB", Pg - 1)] if hasA else []
        S, E, V = [], [], []
        if hasA:
            S.append(("pe", MM(PS[bA][:, 0:512], KT[pr, (Pg - 1) * 128:Pg * 128], qrhs, tp=tpk), ["KT"] + QTALL, [("ps", bA)]))
        S.append(("pe", MM(PS[bB][:, 0:512], KT[pr, Pg * 128:(Pg + 1) * 128], qrhs, tp=tpk), ["KT"] + QTALL, [("ps", bB)]))
        if hasA:
            E.append(("act", [ACT(pa[:, :, 0:64], sa[:, :, 0:64], AF.Exp, scale=0.125),
                              ACT(pa[64:128, :, 64:128], sa[64:128, :, 64:128], AF.Exp, scale=0.125)], [("ps", bA)], ["PA%d" % g]))
        E.append(("act", [ACT(pb[0:64, :, 0:64], sB[0:64, :, 0:64], AF.Exp, scale=0.125),
                          ACT(pb[:, :, 64:128], sB[:, :, 64:128], AF.Exp, scale=0.125)], [("ps", bB)], ["PB%d" % g]))
        ins = []
        if hasA:
            ins.append(MM(PS[6][pr, 0:512], VB[:, Pg - 1, pr], PAB[:, g, 0, :], start=True, stop=False, tp=tpo))
        ins.append(MM(PS[6][pr, 0:512], VB[:, Pg, pr], PAB[:, g, 1, :], start=(not hasA), stop=True, tp=tpo))
        if hasA:
            ins.append(MM(PS[7][pr, 0:512], ONESK, PAB[:, g, 0, :], start=True, stop=False, tp=tpo))
        ins.append(MM(PS[7][pr, 0:512], ONESK, PAB[:, g, 1, :], start=(not hasA), stop=True, tp=tpo))
        V.append(("pe", ins, vA + [("VB", Pg), "CB", "PA%d" % g, "PB%d" % g], [("ps", 6, g), ("ps", 7, g)]))
        return S, E, V

    def ch_attn_norm(C, t, l, pi):
        AT = BQ[1]
        es = ESINK[:, 4 * l:4 * l + 4]
        q0 = pi * 128
        both6 = [("ps", 6, 0), ("ps", 6, 1)]
        both7 = [("ps", 7, 0), ("ps", 7, 1)]
        C.op("dve", TT(v4(RD[:, :]), v4(PS[7][:, :]), es.unsqueeze(2).to_broadcast([128, 4, 128]), ALU.add), r=both7 + ["ESINK"], w=["RD"])
        C.op("act", ACT(RD[:, :], RD[:, :], AF.Ln), r=["RD"], w=["RD"])
        C.op("act", ACT(RD[:, :], RD[:, :], AF.Exp, scale=-1.0), r=["RD"], w=["RD"])
        C.op("dve", TT(AT[:, :, q0:q0 + 128], v4(PS[6][:, :]), v4(RD[:, :]), ALU.mult), r=both6 + ["RD"], w=["AT"])

    def ch_attn_sample(C, t, l):
        QT, AT = BQ[0], BQ[1]
        es = ESINK[:, 4 * l:4 * l + 4]
        v16 = lambda ap: ap.rearrange("p (a b) -> p a b", a=4)
        for g in range(2):
            pr = slice(g * 64, (g + 1) * 64)
            qrhs = QT[pr, :, S5:S5 + 16]
            tpk = (g * 64, 0)
            tpo = (0, g * 64)
            C.op("pe", MM(PS[3][:, 0:64], CKT[pr, l, :], qrhs, tp=tpk), r=["CKT"] + QTALL, w=[("ps", 3)])
            C.op("pe", MM(PS[4][0:16, 0:64], KT[pr, 2304:2320], qrhs, tp=tpk), r=["KT"] + QTALL, w=[("ps", 4)])
            C.op("act", ACT(PAB[:, g, 0, 0:64], PS[3][:, 0:64], AF.Exp, scale=0.125), r=[("ps", 3)], w=["PA%d" % g])
            C.op("act", ACT(PAB[0:16, g, 1, 0:64], PS[4][0:16, 0:64], AF.Exp, scale=0.125), r=[("ps", 4)], w=["PB%d" % g])
            ins = [MM(PS[6][pr, 0:64], CV[:, l, pr], PAB[:, g, 0, 0:64], start=True, stop=False, tp=tpo),
                   MM(PS[6][pr, 0:64], VB[0:16, 18, pr], PAB[0:16, g, 1, 0:64], start=False, stop=True, tp=tpo),
                   MM(PS[7][pr, 0:64], ONESK, PAB[:, g, 0, 0:64], start=True, stop=False, tp=tpo),
                   MM(PS[7][pr, 0:64], ONESK[0:16, :], PAB[0:16, g, 1, 0:64], start=False, stop=True, tp=tpo)]
            C.op("pe", ins, r=[("VB", 18), "CV", "CB", "PA%d" % g, "PB%d" % g], w=[("ps", 6, g), ("ps", 7, g)])
        C.op("dve", TT(v16(RD[:, 0:64]), v16(PS[7][:, 0:64]), es.unsqueeze(2).to_broadcast([128, 4, 16]), ALU.add),
             r=[("ps", 7, 0), ("ps", 7, 1), "ESINK"], w=["RD"])
        C.op("act", ACT(RD[:, 0:64], RD[:, 0:64], AF.Ln), r=["RD"], w=["RD"])
        C.op("act", ACT(RD[:, 0:64], RD[:, 0:64], AF.Exp, scale=-1.0), r=["RD"], w=["RD"])
        C.op("dve", TT(AT[:, :, S5:S5 + 16], v16(PS[6][:, 0:64]), v16(RD[:, 0:64]), ALU.mult), r=[("ps", 6, 0), ("ps", 6, 1), "RD"], w=["AT"])

    def ch_attention(C, t, l):
        got = sorted(C.alloc("st") for _ in range(4))
        assert got == [3, 4, 6, 7], got
        setB = (C.alloc("proj"), C.alloc("proj"))
        sets = [(3, 4), setB]
        units = [attn_unit(t, l, pi, g, sets[g]) for pi in range(nsub(t)) for g in range(2)]

        def emit(lst):
            for eng, insts, r, w in lst:
                C.op(eng, insts, r=r, w=w)
        emit(units[0][0])
        for u in range(len(units)):
            if u + 1 < len(units):
                emit(units[u + 1][0])
            emit(units[u][1])
            if u % 2 == 0 and u >= 2:
                ch_attn_norm(C, t, l, u // 2 - 1)
            emit(units[u][2])
        ch_attn_norm(C, t, l, len(units) // 2 - 1)
        if t == 5:
            ch_attn_sample(C, t, l)

    def ch_pools(C, t, l):
        st, lt, lc0, N = tile_info(t)
        E = 15 + N
        D, PL = BQ[2], BQ[3]
        fss = [C.alloc("fs"), C.alloc("fs"), C.alloc("fs")]
        b = C.peek("proj")
        if t == 5:
            stg = C.alloc("stg")
        for g in range(4):
            bufs = fss[0:2]
            src = None
            for k in range(1, g + 2):
                s_ = 1 << (k - 1)
                lo = (1 << k) - 1
                dst = bufs[(k - 1) % 2]
                if k == 1:
                    C.op("pool", TT(FS[dst][:, lo:E], U[:, g, lo:E], U[:, g, lo - s_:E - s_], ALU.add), r=[("U", g), ("Uc", g)], w=["FS%d" % dst])
                else:
                    C.op("pool", TT(FS[dst][:, lo:E], FS[src][:, lo:E], FS[src][:, lo - s_:E - s_], ALU.add), r=["FS%d" % src], w=["FS%d" % dst])
                src = dst
            wv = float(1 << (g + 1))
            C.op("dve", STT(D[:, g, 0:N], FS[src][:, 15:E], 1.0 / wv, U[:, g, 15:E], ALU.mult, ALU.subtract), r=["FS%d" % src, ("U", g)], w=[("D", g)])
            if t == 0:
                f3 = fss[2]
                C.op("dve", TT(FS[f3][:, 0:16], FS[src][:, 15:31], INVC[:, g * 16:(g + 1) * 16], ALU.mult), r=["FS%d" % src, "CF"], w=["FS%d" % f3])
                C.op("dve", TT(D[:, g, 0:16], FS[f3][:, 0:16], U[:, g, 15:31], ALU.subtract), r=["FS%d" % f3, ("U", g)], w=[("D", g)])
            C.hold()
            C.op("pe", MM(PS[b][:, 0:N], WPOOL[:, g, :], D[:, g, 0:N]), r=["WPOOL", ("D", g)], w=[("ps", b)])
            C.op("act", ACT(PL[:, g, 0:N], PS[b][:, 0:N], AF.Identity, scale=sm(l, O_PSC + g)), r=[("ps", b), "SMALL"], w=[("PL", g)])
            C.unhold()
        if t == 5:
            for c0, dst_ in ((W5, opp), (S5 + 16, ops_)):
                C.hold()
                C.op("pe", [TR(PS[b][0:15, gg * 128:(gg + 1) * 128], U[:, gg, c0:c0 + 15], IDENT) for gg in range(4)],
                     r=[("U", gg) for gg in range(4)] + ["CF"], w=[("ps", b)])
                C.op("act", ACT(STG[stg][0:15, 0:512], PS[b][0:15, 0:512], AF.Copy), r=[("ps", b)], w=["STG%d" % stg])
                C.unhold()
                C.dma("sp", dst_[l, :, :], STG[stg][0:15, 0:512], r=["STG%d" % stg])

    def ch_wout(C, t, mo):
        st, lt, lc0, N = tile_info(t)
        AT, PL = BQ[1], BQ[3]
        b = C.alloc("proj")
        C.op("pe", [MM(PS[b][:, 0:N], WOUT[:, kc, mo * 128:(mo + 1) * 128], (AT[:, kc, 0:N] if kc < 4 else PL[:, kc - 4, 0:N]),
                       start=(kc == 0), stop=(kc == 7)) for kc in range(8)], r=["WOUT", "AT"] + [("PL", g) for g in range(4)], w=[("ps", b)])
        C.op("dve", TT(XT[:, mo, lc0:lc0 + N], PS[b][:, 0:N], XT[:, mo, lc0:lc0 + N], ALU.add), r=[("ps", b), ("XT", lt, mo)], w=[("XT", lt, mo)])

    def ch_prologue(C, t, l):
        ch_rmsnorm(C, t, l * SM_L + O_G1)

    def mixer_tile(t, l, prologue_done, next_t, extra_tail=(), defer_wout=False, pre_wouts=()):
        st, lt, lc0, N = tile_info(t)
        if not prologue_done:
            flush([lambda C: ch_prologue(C, t, l)])
        P.dma("sp", ROPE[:, :, 0:N], rope[:, :, t, 0:N], w=["ROPE"])
        F = lambda fn, *a: (lambda C: fn(C, *a))
        vs = [F(ch_v, t, l, sub) for sub in range(nsub(t) + (1 if t == 5 else 0))]
        us = [F(ch_u, t, l, g) for g in range(4)]
        qs = [F(ch_qk, t, l, j) for j in (4, 0, 1, 2, 3)]
        order = [qs[0], qs[1], vs[0], qs[2], vs[1], qs[3], vs[2], qs[4]] + vs[3:] + [us[0], us[1], us[2], us[3]]
        if pre_wouts:
            merged, pw = [], list(pre_wouts)
            for ch_ in order:
                merged.append(ch_)
                if pw:
                    merged.append(pw.pop(0))
            order = merged + pw
        flush(order, width=4)
        tail = [F(ch_attention, t, l), F(ch_pools, t, l)]
        if next_t is not None:
            tail.append(F(ch_prologue, next_t, l))
        tail += list(extra_tail)
        flush(tail, width=3)
        wouts = [F(ch_wout, t, mo) for mo in range(8)]
        if defer_wout:
            return wouts
        flush(wouts, width=2)

    def out_y(t):
        st, lt, lc0, N = tile_info(t)
        wt = nsub(t) * 128
        for h in range(4):
            P.dma("sp", yp[2 * h:2 * h + 2, :, t * W:t * W + wt].rearrange("k p t -> p k t"), XT[:, 2 * h:2 * h + 2, lc0:lc0 + wt],
                  r=[("XT", lt, kc) for kc in range(2 * h, 2 * h + 2)])
        if t == 5:
            P.dma("sp", ys[:, :, :].rearrange("k p t -> p k t"), XT[:, :, LS:LS + 16], r=[("XT", 2, kc) for kc in range(8)])
        if t < 3:
            load_x(t + 3)

    def ffn_up(C, t, l, grp, slots, mi, only=None):
        _, lt, lc0, N = tile_info(t)
        M = BQ[mi]
        Mname = "BQM%d" % mi
        for si, s in enumerate(grp):
            if only is not None and si not in only:
                continue
            slot = slots[si]
            res = []
            for half in range(2):
                hb = nxt("h", [0, 1, 2, 3])
                c = s + 22 * half
                C.op("pe", [MM(PS[hb][:, 0:N + 2], WUP[:, slot, kc, half * 128:(half + 1) * 128], XN[:, kc, lc0:lc0 + N + 2],
                               start=(kc == 0), stop=(kc == 7)) for kc in range(8)],
                     r=[("WS", slot), ("XN", lt), ("XN", max(lt - 1, 0)), "XNc"], w=[("ps", hb)])
                a = nxt("fs", FSN)
                C.op("act", ACT(FS[a][:, 0:N], PS[hb][:, 2:N + 2], AF.Identity, bias=sm(l, O_CB + c), scale=sm(l, O_CW + 88 + c)),
                     r=[("ps", hb), "SMALL"], w=["FS%d" % a])
                for j in (1, 0):
                    C.op("dve", STT(FS[a][:, 0:N], PS[hb][:, j:N + j], sm(l, O_CW + 44 * j + c), FS[a][:, 0:N], ALU.mult, ALU.add),
                         r=[("ps", hb), "FS%d" % a, "SMALL"], w=["FS%d" % a])
                if t == 5:
                    C.op("dve", TT(FS[a][:, S5:S5 + 2], FS[a][:, S5:S5 + 2], CORR[:, l, c, :], ALU.add), r=["FS%d" % a, "CORR"], w=["FS%d" % a])
                    C.op("act", ACT(HL[:, c, :].rearrange("p (a b) -> p a b", a=2),
                                    PS[hb][:, W5:W5 + 64].rearrange("p (a b) -> p a b", a=2)[:, :, 0:2], AF.Copy), r=[("ps", hb)], w=["HL"])
                res.append(a)
            ag, av = res
            sg = nxt("fs", FSN)
            C.op("act", ACT(FS[sg][:, 0:N], FS[ag][:, 0:N], AF.Silu), r=["FS%d" % ag], w=["FS%d" % sg])
            C.op("pool", TT(M[:, si, 0:N], FS[sg][:, 0:N], FS[av][:, 0:N], ALU.mult), r=["FS%d" % sg, "FS%d" % av], w=[Mname])

    def ffn_down(C, t, l, grp, slots, mi, mos):
        _, lt, lc0, N = tile_info(t)
        M = BQ[mi]
        Mname = "BQM%d" % mi
        for mo in mos:
            yb = nxt("y", [4, 5, 6, 7])
            C.op("pe", [MM(PS[yb][:, 0:N], WDN[:, slots[si], mo * 128:(mo + 1) * 128], M[:, si, 0:N], start=(si == 0), stop=(si == len(grp) - 1))
                        for si in range(len(grp))], r=[("WS", sl) for sl in slots] + [Mname], w=[("ps", yb)])
            if mo % 2 == 0:
                C.op("dve", TT(XT[:, mo, lc0:lc0 + N], PS[yb][:, 0:N], XT[:, mo, lc0:lc0 + N], ALU.add), r=[("ps", yb), ("XT", lt, mo)], w=[("XT", lt, mo)])
            else:
                ya = nxt("fs", FSN)
                C.op("act", ACT(FS[ya][:, 0:N], PS[yb][:, 0:N], AF.Copy), r=[("ps", yb)], w=["FS%d" % ya])
                C.op("pool", TT(XT[:, mo, lc0:lc0 + N], FS[ya][:, 0:N], XT[:, mo, lc0:lc0 + N], ALU.add), r=["FS%d" % ya, ("XT", lt, mo)], w=[("XT", lt, mo)])

    def ffn_pass(st, l, pidx):
        tiles = [3 * st + i for i in range(3)]
        flush([lambda C: ch_rmsnorm(C, tiles[2], l * SM_L + O_G2)])
        if st == 0:
            P.op("dve", CP(CARRY[:, l, :, :], XN[:, :, 1152:1154]), r=[("XN", 2)], w=["CARRY%d" % l])
        else:
            P.op("dve", CP(XN[:, :, 0:2], CARRY[:, l, :, :]), r=["CARRY%d" % l], w=["XNc"])
        units = [(gi, grp, t) for gi, grp in enumerate(GROUPS) for t in tiles]

        def up(ui, only=None):
            gi, grp, t = units[ui]
            slots = [(pidx * 22 + s) % RING for s in grp]
            flush([lambda C: ffn_up(C, t, l, grp, slots, 1 + (ui % 2), only)])

        def down(ui, mos):
            gi, grp, t = units[ui]
            slots = [(pidx * 22 + s) % RING for s in grp]
            flush([lambda C: ffn_down(C, t, l, grp, slots, 1 + (ui % 2), mos)])

        MOS = [[0, 1, 2], [3, 4, 5], [6, 7]]
        up(0)
        for ui, (gi, grp, t) in enumerate(units):
            nslab = len(units[ui + 1][1]) if ui + 1 < len(units) else 0
            for part in range(3):
                if part < nslab:
                    up(ui + 1, only=[part])
                down(ui, MOS[part])
            if gi == len(GROUPS) - 1 and l == 1:
                out_y(t)
            if t == tiles[-1]:
                cap = (pidx + 1) * 22 if (st == 0 and l == 1) else 10 ** 9
                issue_slabs(min(pidx * 22 + grp[-1] + 1 + RING, cap))
        if st == 0 and l == 1:
            issue_slabs((pidx + 1) * 22 + RING, after=[("XT", 0, 7)])
        if st == 1:
            b = nxt("y", [4, 5, 6, 7])
            stg = nxt("stg", [0, 1])
            P.op("pe", [TR(PS[b][0:44, r_ * 128:(r_ + 1) * 128], HL[:, :, r_], IDENT) for r_ in range(4)], r=["HL", "CF"], w=[("ps", b)])
            P.op("act", ACT(STG[stg][0:44, 0:512], PS[b][0:44, 0:512], AF.Copy), r=[("ps", b)], w=["STG%d" % stg])
            for r_ in range(4):
                dst = (ocp if r_ < 2 else ocs)[l, r_ % 2, :].rearrange("(c p) -> c p", p=128)
                P.dma("sp", dst, STG[stg][0:44, r_ * 128:(r_ + 1) * 128], r=["STG%d" % stg])

    for l in range(2):
        stg = nxt("stg", [0, 1])
        P.dma("sp", STG[stg][:, 0:128], ck[l, :, :], w=["STG%d" % stg])
        b = nxt("proj", [0, 1, 2])
        P.op("pe", TR(PS[b][:, 0:128], STG[stg][:, 0:128], IDENT), r=["STG%d" % stg, "CF"], w=[("ps", b)])
        P.op("act", ACT(CKT[:, l, :], PS[b][:, 0:128], AF.Copy), r=[("ps", b)], w=["CKT"])

    load_mixer_weights(0, gate=True)
    pidx = 0
    for st in range(2):
        for l in range(2):
            if st == 1:
                P.op("dve", CP(KT[:, 1024:1152], KC[:, l, :]), r=["KC"], w=["KT"])
                P.op("dve", CP(VB[:, 8, :], VC[:, l, :]), r=["VC"], w=[("VB", 8)])
            tl = list(range(3 * st, 3 * st + 3))
            if l == 0 and st == 0:
                load_x(1, after=["WIN"])
                load_x(2, after=["WIN"])
            for i, t in enumerate(tl):
                if st == 1 and l == 0 and i == 1:
                    P.op("dve", MS(XT[:, :, LG:LG + 16], 0.0), w=[("XT", 2, kc) for kc in range(8)])
                if i < 2:
                    pend = mixer_tile(t, l, prologue_done=(i > 0), next_t=tl[i + 1], defer_wout=True, pre_wouts=(pend if i > 0 else ()))
                else:
                    extra = [(lambda C, t_=t_: ch_rmsnorm(C, t_, l * SM_L + O_G2)) for t_ in tl[0:2]]
                    mixer_tile(t, l, prologue_done=True, next_t=None, extra_tail=extra, pre_wouts=pend)
                if pidx == 0 and i == 0:
                    issue_slabs(RING)
            if st == 0:
                P.op("dve", CP(KC[:, l, :], KT[:, 1024:1152]), r=["KT"], w=["KC"])
                P.op("dve", CP(VC[:, l, :], VB[:, 8, :]), r=[("VB", 8)], w=["VC"])
            if pidx + 1 < 4:
                load_mixer_weights((pidx + 1) % 2)
            ffn_pass(st, l, pidx)
            pidx += 1
    P.finish()
    return nc


def _prep(inputs):
    f = lambda a: np.ascontiguousarray(np.asarray(a, dtype=np.float32))
    x_prompt = f(inputs["x_prompt"]); x_sample = f(inputs["x_sample"])
    cache_k = f(inputs["cache_k"]); cache_v = f(inputs["cache_v"])
    state_pool = f(inputs["state_pool"]); state_conv = f(inputs["state_conv"])
    w_in = f(inputs["w_in"]); w_out = f(inputs["w_out"]); w_pool = f(inputs["w_pool"])
    w_up = f(inputs["w_up"]); w_down = f(inputs["w_down"])
    qperm = np.concatenate([np.concatenate([np.arange(j * 64, j * 64 + 64), np.arange((4 + j) * 64, (4 + j) * 64 + 64)]) for j in range(4)])
    colperm = np.concatenate([qperm, np.arange(512, 640), np.arange(768, 1280), np.arange(640, 768)])
    win = np.ascontiguousarray(w_in[:, :, colperm].reshape(2, 8, 128, 1280).transpose(0, 2, 1, 3))
    rowperm = np.concatenate([qperm, np.arange(512, 1024)])
    wout = np.ascontiguousarray(w_out[:, rowperm, :].reshape(2, 8, 128, 1024).transpose(0, 2, 1, 3))
    wpool = np.ascontiguousarray(w_pool.transpose(0, 2, 1, 3))
    wu = w_up.reshape(2, 8, 128, 2, 22, 128)
    wup = np.ascontiguousarray(wu.transpose(0, 4, 2, 1, 3, 5).reshape(2, 22, 128, 8, 256))
    wdn = np.ascontiguousarray(w_down.reshape(2, 22, 128, 1024))
    small = np.zeros((128, 405), np.float32)
    for l in range(2):
        o = l * SM_L
        small[:, o + O_G1:o + O_G1 + 8] = f(inputs["norm_mix"])[l].reshape(8, 128).T
        small[:, o + O_G2:o + O_G2 + 8] = f(inputs["norm_ffn"])[l].reshape(8, 128).T
        small[:, o + O_GQ] = np.tile(f(inputs["q_norm"])[l], 2)
        small[:, o + O_GK] = np.tile(f(inputs["k_norm"])[l], 2)
        sk = f(inputs["attn_sinks"])[l].reshape(2, 4)
        small[:, o + O_SINK:o + O_SINK + 4] = np.repeat(sk, 64, axis=0)
        small[:, o + O_PSC:o + O_PSC + 4] = f(inputs["pool_scale"])[l].reshape(4, 128).T
        small[:, o + O_CW:o + O_CW + 132] = f(inputs["conv_w"])[l].reshape(3, 44, 128).transpose(2, 0, 1).reshape(128, 132)
        small[:, o + O_CB:o + O_CB + 44] = f(inputs["conv_b"])[l].reshape(44, 128).T
    small[:, O_EPS] = EPS
    constf = np.zeros((128, 192), np.float32)
    constf[:, 0:128] = np.eye(128, dtype=np.float32)
    for g in range(4):
        wv = 2 << g
        constf[:, 128 + g * 16:128 + (g + 1) * 16] = 1.0 / np.minimum(np.arange(16) + 1, wv).astype(np.float32)
    constb = np.zeros((128, 448), np.float32)
    constb[:, 0:128] = 1.0 / 1024.0
    constb[0:64, 128:192] = 1.0 / 64.0
    constb[64:128, 192:256] = 1.0 / 64.0
    d = np.arange(128) % 64
    for m in range(128):
        if d[m] < 8:
            constb[m + 8, 256 + m] = 1.0
        elif d[m] < 16:
            constb[m - 8, 256 + m] = 1.0
    constb[:, 384:448] = 1.0
    inv = (500000.0 ** (-np.arange(0, 16, 2, dtype=np.float32) / 16.0)).astype(np.float32)
    in_maps = []
    for c in range(NCORES):
        s, hf = divmod(c, 2)
        t0 = 0 if hf == 0 else 4096 - TCORE
        pos = np.zeros((NTILE, NX), np.float32)
        for t in range(NTILE):
            pos[t, 0:W] = t0 + t * W + np.arange(W)
        pos[5, S5:S5 + 16] = 4096 + np.arange(16)
        ang = pos[None, :, :] * inv[:, None, None]
        cosv, sinv = np.cos(ang).astype(np.float32), np.sin(ang).astype(np.float32)
        rope = np.zeros((128, 2, NTILE, NX), np.float32)
        rope[:, 0] = 1.0
        for p in range(128):
            dd = p % 64
            if dd < 8:
                rope[p, 0] = cosv[dd]; rope[p, 1] = -sinv[dd]
            elif dd < 16:
                rope[p, 0] = cosv[dd - 8]; rope[p, 1] = sinv[dd - 8]
        in_maps.append({
            "xp": np.ascontiguousarray(x_prompt[s, t0:t0 + TCORE, :].T).reshape(8, 128, TCORE),
            "xs": np.ascontiguousarray(x_sample[c].T).reshape(8, 128, 16),
            "ck": np.ascontiguousarray(cache_k[:, c].reshape(2, 128, 128)),
            "cv": np.ascontiguousarray(cache_v[:, c].reshape(2, 128, 128)),
            "spool": np.ascontiguousarray(state_pool[:, c]),
            "sconv": np.ascontiguousarray(state_conv[:, c]),
            "win": win, "wout": wout, "wpool": wpool, "wup": wup, "wdn": wdn,
            "small": small, "constf": constf, "constb": constb, "rope": rope,
        })
    return in_maps


_NC_CACHE = {}


def kernel(**inputs):
    in_maps = _prep(inputs)
    if "nc" not in _NC_CACHE:
        _NC_CACHE["nc"] = build_nc()
    nc = _NC_CACHE["nc"]
    res = run_bass_kernel_spmd(nc, in_maps, core_ids=list(range(NCORES)))
    R = res.results
    y_prompt = np.zeros((4, 4096, 1024), np.float32)
    for s in range(4):
        y_prompt[s, 0:TCORE] = R[2 * s]["yp"].reshape(1024, TCORE).T
        y_prompt[s, TCORE:4096] = R[2 * s + 1]["yp"].reshape(1024, TCORE).T[2 * TCORE - 4096:]
    y_sample = np.stack([R[c]["ys"].reshape(1024, 16).T for c in range(8)])
    pk = lambda name, shape: np.stack([np.stack([R[2 * s + 1][name][l] for s in range(4)]) for l in range(2)]).reshape(shape)
    sk = lambda name, shape: np.stack([np.stack([R[c][name][l] for c in range(8)]) for l in range(2)]).reshape(shape)
    return (y_prompt, y_sample,
            pk("okp", (2, 4, 128, 2, 64)), pk("ovp", (2, 4, 128, 2, 64)), pk("opp", (2, 4, 15, 512)), pk("ocp", (2, 4, 2, 5632)),
            sk("oks", (2, 8, 128, 2, 64)), sk("ovs", (2, 8, 128, 2, 64)), sk("ops", (2, 8, 15, 512)), sk("ocs", (2, 8, 2, 5632)))
```

```python
import os
import numpy as np
import concourse.bass as bass
import concourse.mybir as mybir
from concourse.bass_utils import run_bass_kernel_spmd

F32 = mybir.dt.float32
BF16 = mybir.dt.bfloat16
AF = mybir.ActivationFunctionType
ALU = mybir.AluOpType

NCORES = 8
TCORE = 2176
W = 384
NTILE = 6
LW = 1184
NX = 416
W5 = 256
S5 = W5 + 16
N5 = W5 + 32
LG = 2 * W + W5
LS = LG + 16
G = 3
RING = 2 * G
GROUPS = [list(range(i, min(i + G, 22))) for i in range(0, 22, G)]
EPS = 1e-6
SM_L = 202
O_G1, O_G2, O_GQ, O_GK, O_SINK, O_PSC, O_CW, O_CB = 0, 8, 16, 17, 18, 22, 26, 158
O_EPS = 404


class Prog:
    def __init__(self, nc):
        self.nc = nc
        self.names = ["pe", "act", "dve", "pool", "sp"]
        self.eng = {"pe": nc.tensor, "act": nc.scalar, "dve": nc.vector, "pool": nc.gpsimd, "sp": nc.sync}
        self.sem = {n: nc.alloc_semaphore("s_" + n) for n in self.names}
        self.cnt = {n: 0 for n in self.names}
        self.lists = {n: [] for n in self.names}
        self.seen = {n: {} for n in self.names}
        self.lastw = {}
        self.readers = {}
        self.ndsem = 56
        self.dsem = [nc.alloc_semaphore("d%d" % i) for i in range(self.ndsem)]
        self.dval = [0] * self.ndsem
        self.dnext = 0
        self.alias = {}

    def _wait(self, eng, tok):
        if tok is None:
            return
        kind, key, val = tok
        if kind == "e" and key == eng and eng in ("pe", "sp"):
            return
        k = (kind, key)
        if self.seen[eng].get(k, 0) >= val:
            return
        self.seen[eng][k] = val
        sem = self.sem[key] if kind == "e" else self.dsem[key]
        self.lists[eng].append(lambda e, sem=sem, val=val: e.wait_ge(sem, val))

    def _canon(self, regs):
        out = []
        for reg in regs:
            out.extend(self.alias.get(reg, [reg]))
        return out

    def _deps(self, eng, r, w, is_dma=False):
        r, w = self._canon(r), self._canon(w)
        for reg in r:
            for tok in self.lastw.get(reg, ()):
                self._wait(eng, tok)
        for reg in w:
            for tok in self.lastw.get(reg, ()):
                if is_dma and tok[0] == "d" and not self.readers.get(reg):
                    continue
                self._wait(eng, tok)
            for tok in self.readers.get(reg, ()):
                self._wait(eng, tok)

    def _commit(self, tok, r, w):
        r, w = self._canon(r), self._canon(w)
        for reg in r:
            self.readers.setdefault(reg, []).append(tok)
        for reg in w:
            prev = self.lastw.get(reg, [])
            if tok[0] == "d" and prev and all(p[0] == "d" for p in prev) and not self.readers.get(reg):
                self.lastw[reg] = prev + [tok]
            else:
                self.lastw[reg] = [tok]
            self.readers[reg] = []

    def op(self, eng, insts, r=(), w=()):
        if isinstance(insts, tuple):
            insts = [insts]
        self._deps(eng, r, w)
        self.cnt[eng] += 1
        sem = self.sem[eng]

        def run(e, insts=insts, sem=sem):
            ins = None
            for meth, kw in insts:
                ins = getattr(e, meth)(**kw)
            ins.then_inc(sem, 1)
        self.lists[eng].append(run)
        self._commit(("e", eng, self.cnt[eng]), r, w)

    def dma(self, q, out, in_, r=(), w=(), **kw):
        self._deps(q, r, w, is_dma=True)
        i = self.dnext
        self.dnext = (self.dnext + 1) % self.ndsem
        if self.dval[i] > 0:
            self._wait(q, ("d", i, self.dval[i]))
        self.dval[i] += 16
        sem = self.dsem[i]
        self.lists[q].append(lambda e, out=out, in_=in_, sem=sem, kw=kw: e.dma_start(out=out, in_=in_, **kw).then_inc(sem, 16))
        self._commit(("d", i, self.dval[i]), r, w)

    def finish(self):
        for i in range(self.ndsem):
            if self.dval[i] > 0:
                self._wait("sp", ("d", i, self.dval[i]))
        for n in self.names:
            if n != "sp" and self.cnt[n] > 0:
                self._wait("sp", ("e", n, self.cnt[n]))
        with self.nc.Block() as block:
            def run(name):
                def f(e):
                    for fn in self.lists[name]:
                        fn(e)
                return f
            block.sync(run("sp"))
            block.tensor(run("pe"))
            block.scalar(run("act"))
            block.vector(run("dve"))
            block.gpsimd(run("pool"))


def build_nc():
    nc = bass.Bass("TRN2", target_bir_lowering=False)
    P = Prog(nc)

    def din(name, shape):
        return nc.dram_tensor(name, list(shape), F32, kind="ExternalInput").ap()

    def dout(name, shape):
        return nc.dram_tensor(name, list(shape), F32, kind="ExternalOutput").ap()

    xp = din("xp", [8, 128, TCORE]); xs = din("xs", [8, 128, 16])
    ck = din("ck", [2, 128, 128]); cv = din("cv", [2, 128, 128])
    spool = din("spool", [2, 15, 512]); sconv = din("sconv", [2, 2, 5632])
    win = din("win", [2, 128, 8, 1280]); wout = din("wout", [2, 128, 8, 1024]); wpool = din("wpool", [2, 128, 4, 128])
    wup = din("wup", [2, 22, 128, 8, 256]); wdn = din("wdn", [2, 22, 128, 1024])
    small = din("small", [128, 405]); constf = din("constf", [128, 192]); constb = din("constb", [128, 448])
    rope = din("rope", [128, 2, NTILE, NX])
    yp = dout("yp", [8, 128, TCORE]); ys = dout("ys", [8, 128, 16])
    okp = dout("okp", [2, 128, 128]); ovp = dout("ovp", [2, 128, 128]); opp = dout("opp", [2, 15, 512]); ocp = dout("ocp", [2, 2, 5632])
    oks = dout("oks", [2, 128, 128]); ovs = dout("ovs", [2, 128, 128]); ops_ = dout("ops", [2, 15, 512]); ocs = dout("ocs", [2, 2, 5632])

    sb = nc.alloc_sbuf_tensor
    XT = sb("XT", [128, 8, LW], F32)
    XN = sb("XN", [128, 8, LW + 2], BF16)
    KT = sb("KT", [128, 2320], BF16)
    VB = sb("VB", [128, 19, 128], BF16)
    KC = sb("KC", [128, 2, 128], BF16); VC = sb("VC", [128, 2, 128], BF16)
    WIN = sb("WIN", [128, 8, 1280], BF16); WOUT = sb("WOUT", [128, 8, 1024], BF16); WPOOL = sb("WPOOL", [128, 4, 128], BF16)
    WUP = sb("WUP", [128, RING, 8, 256], BF16); WDN = sb("WDN", [128, RING, 1024], BF16)
    SMALL = sb("SMALL", [128, 405], F32); CF = sb("CF", [128, 192], F32); CB = sb("CB", [128, 448], BF16)
    ROPE = sb("ROPE", [128, 2, NX], F32)
    U = sb("U", [128, 4, 15 + NX], F32)
    FS = [sb("FS%d" % i, [128, 432], F32) for i in range(8)]
    BQ = [sb("BQ%d" % i, [128, 4, NX], BF16) for i in range(4)]
    BS = [sb("BS%d" % i, [128, NX], BF16) for i in range(6)]
    PAB = sb("PAB", [128, 2, 2, 512], BF16)
    RD = sb("RD", [128, 512], F32)
    STG = [sb("STG%d" % i, [128, 1024], F32) for i in range(2)]
    ESINK = sb("ESINK", [128, 8], F32)
    UC = sb("UC", [128, 2, 4, 15], F32); CARRY = sb("CARRY", [128, 2, 8, 2], BF16)
    SPT = sb("SPT", [128, 2, 4, 15], F32); SCT = sb("SCT", [128, 2, 2, 44], F32); CORR = sb("CORR", [128, 2, 44, 2], F32)
    HL = sb("HL", [128, 44, 4], F32); TMPC = sb("TMPC", [128, 44], F32)
    CKT = sb("CKT", [128, 2, 128], BF16); CV = sb("CV", [128, 2, 128], BF16)
    PS = [nc.alloc_psum_tensor("PS%d" % i, [128, 512], F32) for i in range(8)]

    IDENT = CF[:, 0:128]
    INVC = CF[:, 128:192]
    ONES_D = CB[:, 0:128]; BLK = CB[:, 128:256]; PERM = CB[:, 256:384]; ONESK = CB[:, 384:448]
    EPSC = SMALL[:, O_EPS:O_EPS + 1]

    def sm(l, off, n=1):
        return SMALL[:, l * SM_L + off: l * SM_L + off + n]

    rot = {}

    def nxt(key, choices):
        v = choices[rot.get(key, 0) % len(choices)]
        rot[key] = rot.get(key, 0) + 1
        return v

    def MM(out, lhsT, rhs, start=True, stop=True, tp=None):
        kw = dict(out=out, lhsT=lhsT, rhs=rhs, start=start, stop=stop)
        if tp is not None:
            kw["tile_position"] = tp
        return ("matmul", kw)

    def TR(out, in_, ident):
        return ("transpose", dict(out=out, in_=in_, identity=ident))

    def ACT(out, in_, func, bias=None, scale=None):
        kw = dict(out=out, in_=in_, func=func)
        if bias is not None:
            kw["bias"] = bias
        if scale is not None:
            kw["scale"] = scale
        return ("activation", kw)

    def TT(out, in0, in1, op):
        return ("tensor_tensor", dict(out=out, in0=in0, in1=in1, op=op))

    def STT(out, in0, scalar, in1, op0, op1):
        return ("scalar_tensor_tensor", dict(out=out, in0=in0, scalar=scalar, in1=in1, op0=op0, op1=op1))

    def CP(out, in_):
        return ("tensor_copy", dict(out=out, in_=in_))

    def MS(ap, v):
        return ("memset", dict(ap=ap, constant=v))

    def RCP(out, in_):
        return ("reciprocal", dict(out=out, in_=in_))

    def v4(ap):
        return ap.rearrange("p (a b) -> p a b", a=4)

    FSN = list(range(8))

    P.dma("sp", SMALL[:], small[:, :], w=["SMALL"])
    for h in range(4):
        P.dma("act", XT[:, 2 * h:2 * h + 2, 0:W], xp[2 * h:2 * h + 2, :, 0:W].rearrange("k p t -> p k t"),
              w=[("XT", 0, kc) for kc in range(2 * h, 2 * h + 2)])
    P.dma("sp", CF[:], constf[:, :], w=["CF"])
    P.dma("pool", CB[:], constb[:, :], w=["CB"])
    for l in range(2):
        P.dma("pool", CV[:, l, :], cv[l, :, :], w=["CV"])
    P.op("dve", MS(PAB[:], 0.0), w=["PA0", "PA1", "PB0", "PB1"])
    P.op("dve", MS(XN[:, :, 0:2], 0.0), w=["XNc"])
    P.op("dve", MS(UC[:], 0.0), w=[("UC", l, g) for l in range(2) for g in range(4)])
    P.alias[("ps", 6)] = [("ps", 6, 0), ("ps", 6, 1)]
    P.alias[("ps", 7)] = [("ps", 7, 0), ("ps", 7, 1)]
    P.alias["BQM1"] = ["AT"]
    P.alias["BQM2"] = [("D", g) for g in range(4)]
    bq_regions = [[("QT", j) for j in range(4)], ["AT"], [("D", g) for g in range(4)], [("PL", g) for g in range(4)]]
    for i in range(4):
        P.op("dve", MS(BQ[i][:], 0.0), w=bq_regions[i])
    P.op("dve", MS(U[:], 0.0), w=[("U", g) for g in range(4)] + [("Uc", g) for g in range(4)])
    P.op("dve", MS(HL[:], 0.0), w=["HL"])
    for l in range(2):
        P.op("act", ACT(ESINK[:, 4 * l:4 * l + 4], sm(l, O_SINK, 4), AF.Exp), r=["SMALL"], w=["ESINK"])
    for l in range(2):
        stg = nxt("stg", [0, 1])
        P.dma("sp", STG[stg][0:15, 0:512], spool[l, :, :], w=["STG%d" % stg])
        P.op("pe", [TR(PS[0][:, g * 16:g * 16 + 15], STG[stg][0:15, g * 128:(g + 1) * 128], CF[0:15, 0:15]) for g in range(4)],
             r=["STG%d" % stg, "CF"], w=[("ps", 0)])
        P.op("act", ACT(SPT[:, l, :, :], PS[0][:, 0:64].rearrange("p (a b) -> p a b", a=4)[:, :, 0:15], AF.Copy), r=[("ps", 0)], w=["SPT"])
        stg = nxt("stg", [0, 1])
        P.dma("sp", STG[stg][0:88, 0:128], sconv[l, :, :].rearrange("r (c p) -> (r c) p", p=128), w=["STG%d" % stg])
        P.op("pe", TR(PS[1][:, 0:88], STG[stg][0:88, 0:128], CF[0:88, 0:88]), r=["STG%d" % stg, "CF"], w=[("ps", 1)])
        P.op("act", ACT(SCT[:, l, :, :].rearrange("p a b -> p (a b)"), PS[1][:, 0:88], AF.Copy), r=[("ps", 1)], w=["SCT"])
    for l in range(2):
        cw0 = sm(l, O_CW, 44)
        cw1 = sm(l, O_CW + 44, 44)
        P.op("dve", TT(CORR[:, l, :, 1], cw0, SCT[:, l, 1, :], ALU.mult), r=["SMALL", "SCT"], w=["CORR"])
        P.op("dve", TT(TMPC[:], cw1, SCT[:, l, 1, :], ALU.mult), r=["SMALL", "SCT"], w=["TMPC"])
        P.op("dve", TT(CORR[:, l, :, 0], cw0, SCT[:, l, 0, :], ALU.mult), r=["SMALL", "SCT"], w=["CORR"])
        P.op("dve", TT(CORR[:, l, :, 0], CORR[:, l, :, 0], TMPC[:], ALU.add), r=["TMPC", "CORR"], w=["CORR"])

    def load_mixer_weights(l, gate=False):
        after = ["WIN"] if gate else []
        for h in range(4):
            P.dma("pool", WIN[:, 2 * h:2 * h + 2, :], win[l, :, 2 * h:2 * h + 2, :], w=["WIN"])
        for h in range(2):
            P.dma("pool", WOUT[:, 4 * h:4 * h + 4, :], wout[l, :, 4 * h:4 * h + 4, :], r=after, w=["WOUT"])
        P.dma("pool", WPOOL[:], wpool[l, :, :, :], r=after, w=["WPOOL"])

    slab_q = {"issued": 0}
    passes = [(st, l) for st in range(2) for l in range(2)]

    def issue_slabs(upto):
        while slab_q["issued"] < min(upto, len(passes) * 22):
            q = slab_q["issued"]
            l = passes[q // 22][1]
            s = q % 22
            slot = q % RING
            P.dma("pool", WUP[:, slot, :, :], wup[l, s, :, :, :], w=[("WS", slot)])
            P.dma("pool", WDN[:, slot, :], wdn[l, s, :, :], w=[("WS", slot)])
            slab_q["issued"] += 1

    def tile_info(t):
        st, lt = divmod(t, 3)
        N = N5 if t == 5 else W
        return st, lt, lt * W, N

    def nsub(t):
        return 2 if t == 5 else 3

    class NoRes(Exception):
        pass

    free_lists = {"st": [3, 4, 6, 7], "proj": [0, 1, 2], "nb": [5], "bs": list(range(len(BS))), "fs": list(range(8)), "stg": [0, 1]}

    class Chain:
        def __init__(self):
            self.items = []
            self.owned = []

        def op(self, eng, insts, r=(), w=()):
            self.items.append(("op", (eng, insts), dict(r=list(r), w=list(w))))

        def dma(self, q, out, in_, r=(), w=(), **kw):
            self.items.append(("dma", (q, out, in_), dict(r=list(r), w=list(w), **kw)))

        def alloc(self, kind):
            fl = free_lists[kind]
            if not fl:
                raise NoRes(kind)
            x = fl.pop(0)
            self.owned.append((kind, x))
            return x

        def free(self, kind, x):
            self.items.append(("free", (kind, x), {}))

        def peek(self, kind):
            fl = free_lists[kind]
            if not fl:
                raise NoRes(kind)
            return fl[0]

        def hold(self):
            self.items.append(("hold", (), {}))

        def unhold(self):
            self.items.append(("unhold", (), {}))

    sim = {"eng": {n: 0.0 for n in P.names}, "now": 0.0, "reg": {}}

    def regs_ready(kw):
        t_ = 0.0
        for reg in P._canon(list(kw.get("r", ())) + list(kw.get("w", ()))):
            t_ = max(t_, sim["reg"].get(reg, 0.0))
        return t_

    def op_cost(kind, a):
        if kind == "dma":
            return 60.0, 2000.0
        eng, insts = a
        if isinstance(insts, tuple):
            insts = [insts]
        tot = 0.0
        for meth, kw in insts:
            o = kw.get("out", kw.get("ap"))
            n = 1
            for d_ in o.shape[1:]:
                n *= d_
            if eng == "pe":
                tot += 20 + n / (1.2 if "tile_position" in kw else 2.3)
            elif eng == "act":
                tot += 150 + 0.8 * n
            elif eng == "dve":
                tot += 80 + 1.15 * n
            else:
                tot += 50 + 2.1 * n
        return tot, tot + 100.0

    def flush(factories, width=3):
        if os.environ.get("K_NOILV") is not None:
            width = 1
        pending = list(factories)
        active = []
        pos = {}
        ready = {}

        def release(c, kind, x):
            c.owned.remove((kind, x))
            free_lists[kind].append(x)

        def try_start():
            while pending and len(active) < width:
                c = Chain()
                try:
                    pending[0](c)
                except NoRes:
                    for kind, x in c.owned:
                        free_lists[kind].insert(0, x)
                    if not active:
                        raise
                    return
                pending.pop(0)
                if c.items:
                    active.append(c)
                    pos[id(c)] = 0
                    ready[id(c)] = sim["now"]

        def head(c):
            kind, a, kw = c.items[pos[id(c)]]
            return kind, a, kw

        def emit_one(c):
            kind, a, kw = head(c)
            eng = a[0]
            busy, lat = op_cost(kind, a)
            start = max(sim["eng"][eng], ready[id(c)], regs_ready(kw))
            (P.op if kind == "op" else P.dma)(*a, **kw)
            sim["eng"][eng] = start + busy
            ready[id(c)] = start + lat
            for reg in P._canon(list(kw.get("w", ()))):
                sim["reg"][reg] = start + lat
            sim["now"] = max(sim["now"], start)
            pos[id(c)] += 1
            return True

        def skip_markers(c, held):
            while pos[id(c)] < len(c.items):
                kind, a, kw = c.items[pos[id(c)]]
                if kind == "free":
                    release(c, *a)
                elif kind == "hold":
                    held = True
                elif kind == "unhold":
                    held = False
                else:
                    return True, held
                pos[id(c)] += 1
            active.remove(c)
            for kind2, x in list(c.owned):
                release(c, kind2, x)
            return False, False

        def start_chains():
            n0 = len(active)
            try_start()
            for c in list(active[n0:]):
                skip_markers(c, False)

        start_chains()
        while active:
            best, best_t = None, None
            for c in active:
                kind, a, kw = head(c)
                t_ = max(sim["eng"][a[0]], ready[id(c)], regs_ready(kw))
                if best is None or t_ < best_t:
                    best, best_t = c, t_
            c = best
            held = False
            while True:
                emit_one(c)
                alive, held = skip_markers(c, held)
                if not alive or not held:
                    break
            start_chains()

    def load_x(t, queues=("sp",), after=()):
        if True:
            _, lt, lc0, N = tile_info(t)
            wt = nsub(t) * 128
            for h in range(4):
                P.dma(queues[h % len(queues)], XT[:, 2 * h:2 * h + 2, lc0:lc0 + wt], xp[2 * h:2 * h + 2, :, t * W:t * W + wt].rearrange("k p t -> p k t"),
                      r=list(after), w=[("XT", lt, kc) for kc in range(2 * h, 2 * h + 2)])
            if t == 5:
                P.dma("sp", XT[:, :, LS:LS + 16], xs[:, :, :].rearrange("k p t -> p k t"), w=[("XT", 2, kc) for kc in range(8)])

    def ch_rmsnorm(C, t, gcol):
        st, lt, lc0, N = tile_info(t)
        nb = C.alloc("nb")
        sqs = [C.alloc("bs"), C.alloc("bs")]
        fs = C.alloc("fs")
        for kc in range(8):
            sq = sqs[kc % 2]
            C.op("act", ACT(BS[sq][:, 0:N], XT[:, kc, lc0:lc0 + N], AF.Square), r=[("XT", lt, kc)], w=["BS%d" % sq])
            C.op("pe", MM(PS[nb][:, 0:N], ONES_D, BS[sq][:, 0:N], start=(kc == 0), stop=(kc == 7)), r=["BS%d" % sq, "CB"], w=[("ps", nb)])
        C.free("bs", sqs[0])
        C.free("bs", sqs[1])
        C.op("act", ACT(FS[fs][:, 0:N], PS[nb][:, 0:N], AF.Ln, bias=EPSC, scale=1.0), r=[("ps", nb), "SMALL"], w=["FS%d" % fs])
        C.op("act", ACT(FS[fs][:, 0:N], FS[fs][:, 0:N], AF.Exp, scale=-0.5), r=["FS%d" % fs], w=["FS%d" % fs])
        for kc in range(8):
            C.op("dve", STT(XN[:, kc, 2 + lc0:2 + lc0 + N], XT[:, kc, lc0:lc0 + N], SMALL[:, gcol + kc:gcol + kc + 1], FS[fs][:, 0:N], ALU.mult, ALU.mult),
                 r=[("XT", lt, kc), "FS%d" % fs, "SMALL"], w=[("XN", lt)])
        if t == 5:
            C.op("dve", MS(XN[:, :, 2 + LG:2 + LS], 0.0), w=[("XN", 2)])

    def proj_op(C, t, col0, b):
        st, lt, lc0, N = tile_info(t)
        C.op("pe", [MM(PS[b][:, 0:N], WIN[:, kc, col0:col0 + 128], XN[:, kc, 2 + lc0:2 + lc0 + N], start=(kc == 0), stop=(kc == 7)) for kc in range(8)],
             r=["WIN", ("XN", lt)], w=[("ps", b)])

    def ch_qk(C, t, l, j):
        st, lt, lc0, N = tile_info(t)
        b = C.alloc("proj")
        sq = C.alloc("bs")
        qg = C.alloc("bs")
        fr = C.alloc("fs")
        f1 = C.alloc("fs")
        f2 = C.alloc("fs")
        mb = C.alloc("st")
        wb = C.alloc("st")
        if j == 4 and t == 5:
            b2 = C.alloc("proj")
            stgs = [C.alloc("stg"), C.alloc("stg")]
        proj_op(C, t, j * 128, b)
        gcol = sm(l, O_GQ if j < 4 else O_GK)
        C.op("act", ACT(BS[sq][:, 0:N], PS[b][:, 0:N], AF.Square), r=[("ps", b)], w=["BS%d" % sq])
        C.op("act", ACT(BS[qg][:, 0:N], PS[b][:, 0:N], AF.Identity, scale=gcol), r=[("ps", b), "SMALL"], w=["BS%d" % qg])
        C.free("proj", b)
        C.op("pe", MM(PS[mb][:, 0:N], BLK, BS[sq][:, 0:N]), r=["BS%d" % sq, "CB"], w=[("ps", mb)])
        C.free("bs", sq)
        C.op("pe", MM(PS[wb][:, 0:N], PERM, BS[qg][:, 0:N]), r=["BS%d" % qg, "CB"], w=[("ps", wb)])
        C.op("act", ACT(FS[fr][:, 0:N], PS[mb][:, 0:N], AF.Ln, bias=EPSC, scale=1.0), r=[("ps", mb), "SMALL"], w=["FS%d" % fr])
        C.op("act", ACT(FS[fr][:, 0:N], FS[fr][:, 0:N], AF.Exp, scale=-0.5), r=["FS%d" % fr], w=["FS%d" % fr])
        C.free("st", mb)
        C.op("dve", TT(FS[f1][:, 0:N], PS[wb][:, 0:N], ROPE[:, 1, 0:N], ALU.mult), r=[("ps", wb), "ROPE"], w=["FS%d" % f1])
        C.free("st", wb)
        C.op("dve", TT(FS[f2][:, 0:N], BS[qg][:, 0:N], ROPE[:, 0, 0:N], ALU.mult), r=["BS%d" % qg, "ROPE"], w=["FS%d" % f2])
        C.free("bs", qg)
        C.op("pool", TT(FS[f1][:, 0:N], FS[f1][:, 0:N], FS[f2][:, 0:N], ALU.add), r=["FS%d" % f1, "FS%d" % f2], w=["FS%d" % f1])
        if j < 4:
            C.op("dve", TT(BQ[0][:, j, 0:N], FS[f1][:, 0:N], FS[fr][:, 0:N], ALU.mult), r=["FS%d" % f1, "FS%d" % fr], w=[("QT", j)])
            return
        C.op("dve", TT(FS[f1][:, 0:N], FS[f1][:, 0:N], FS[fr][:, 0:N], ALU.mult), r=["FS%d" % f1, "FS%d" % fr], w=["FS%d" % f1])
        cc0 = t * W
        wk = W5 if t == 5 else W
        C.op("act", ACT(KT[:, cc0:cc0 + wk], FS[f1][:, 0:wk], AF.Copy), r=["FS%d" % f1], w=["KT"])
        if t == 5:
            C.op("act", ACT(KT[:, 2304:2320], FS[f1][:, S5:S5 + 16], AF.Copy), r=["FS%d" % f1], w=["KT"])
            for i_, (src, nrows, dst) in enumerate(((FS[f1][:, W5 - 128:W5], 128, okp[l, :, :]), (FS[f1][:, S5:S5 + 16], 16, oks[l, 112:128, :]))):
                stg = stgs[i_]
                C.op("pe", TR(PS[b2][0:nrows, 0:128], src, IDENT), r=["FS%d" % f1, "CF"], w=[("ps", b2)])
                C.op("act", ACT(STG[stg][0:nrows, 0:128], PS[b2][0:nrows, 0:128], AF.Copy), r=[("ps", b2)], w=["STG%d" % stg])
                C.dma("sp", dst, STG[stg][0:nrows, 0:128], r=["STG%d" % stg])
            C.dma("sp", oks[l, 0:112, :], ck[l, 16:128, :])

    def ch_u(C, t, l, g):
        st, lt, lc0, N = tile_info(t)
        b = C.alloc("proj")
        C.op("dve", CP(U[:, g, 0:15], UC[:, l, g, :]), r=[("UC", l, g)], w=[("Uc", g)])
        proj_op(C, t, 640 + g * 128, b)
        C.op("act", ACT(U[:, g, 15:15 + N], PS[b][:, 0:N], AF.Copy), r=[("ps", b)], w=[("U", g)])
        if t == 5:
            C.op("dve", CP(U[:, g, S5:S5 + 15], SPT[:, l, g, :]), r=["SPT"], w=[("U", g)])
        C.op("dve", CP(UC[:, l, g, :], U[:, g, W:W + 15]), r=[("U", g)], w=[("UC", l, g)])

    def ch_v(C, t, l, sub):
        st, lt, lc0, N = tile_info(t)
        ns = nsub(t)
        m, c0, slot = (128, lc0 + sub * 128, t * 3 + sub) if sub < ns else (16, LS, 18)
        b = C.alloc("proj")
        if t == 5 and sub >= ns - 1:
            stg = C.alloc("stg")
        C.op("pe", [MM(PS[b][0:m, 0:128], XN[:, kc, 2 + c0:2 + c0 + m], WIN[:, kc, 1152:1280], start=(kc == 0), stop=(kc == 7)) for kc in range(8)],
             r=["WIN", ("XN", lt)], w=[("ps", b)])
        C.op("act", ACT(VB[0:m, slot, :], PS[b][0:m, 0:128], AF.Copy), r=[("ps", b)], w=[("VB", slot)])
        if t == 5 and sub >= ns - 1:
            C.op("act", ACT(STG[stg][0:m, 0:128], PS[b][0:m, 0:128], AF.Copy), r=[("ps", b)], w=["STG%d" % stg])
            if sub == ns - 1:
                C.dma("sp", ovp[l, :, :], STG[stg][0:128, 0:128], r=["STG%d" % stg])
            else:
                C.dma("sp", ovs[l, 112:128, :], STG[stg][0:16, 0:128], r=["STG%d" % stg])
                C.dma("sp", ovs[l, 0:112, :], cv[l, 16:128, :])

    QTALL = [("QT", j) for j in range(4)]

    def attn_unit(t, l, pi, g, sbank):
        QT = BQ[0]
        Pg = t * 3 + pi
        q0 = pi * 128
        hasA = Pg > 0
        pr = slice(g * 64, (g + 1) * 64)
        qrhs = QT[pr, :, q0:q0 + 128]
        tpk = (g * 64, 0)
        tpo = (0, g * 64)
        pa = v4(PAB[:, g, 0, :])
        pb = v4(PAB[:, g, 1, :])
        bA, bB = sbank
        sa = v4(PS[bA][:, :])
        sB = v4(PS[bB][:, :])
        vA = [("VB", Pg - 1)] if hasA else []
        S, E, V = [], [], []
        if hasA:
            S.append(("pe", MM(PS[bA][:, 0:512], KT[pr, (Pg - 1) * 128:Pg * 128], qrhs, tp=tpk), ["KT"] + QTALL, [("ps", bA)]))
        S.append(("pe", MM(PS[bB][:, 0:512], KT[pr, Pg * 128:(Pg + 1) * 128], qrhs, tp=tpk), ["KT"] + QTALL, [("ps", bB)]))
        if hasA:
            E.append(("act", [ACT(pa[:, :, 0:64], sa[:, :, 0:64], AF.Exp, scale=0.125),
                              ACT(pa[64:128, :, 64:128], sa[64:128, :, 64:128], AF.Exp, scale=0.125)], [("ps", bA)], ["PA%d" % g]))
        E.append(("act", [ACT(pb[0:64, :, 0:64], sB[0:64, :, 0:64], AF.Exp, scale=0.125),
                          ACT(pb[:, :, 64:128], sB[:, :, 64:128], AF.Exp, scale=0.125)], [("ps", bB)], ["PB%d" % g]))
        ins = []
        if hasA:
            ins.append(MM(PS[6][pr, 0:512], VB[:, Pg - 1, pr], PAB[:, g, 0, :], start=True, stop=False, tp=tpo))
        ins.append(MM(PS[6][pr, 0:512], VB[:, Pg, pr], PAB[:, g, 1, :], start=(not hasA), stop=True, tp=tpo))
        if hasA:
            ins.append(MM(PS[7][pr, 0:512], ONESK, PAB[:, g, 0, :], start=True, stop=False, tp=tpo))
        ins.append(MM(PS[7][pr, 0:512], ONESK, PAB[:, g, 1, :], start=(not hasA), stop=True, tp=tpo))
        V.append(("pe", ins, vA + [("VB", Pg), "CB", "PA%d" % g, "PB%d" % g], [("ps", 6, g), ("ps", 7, g)]))
        return S, E, V

    def ch_attn_norm(C, t, l, pi):
        AT = BQ[1]
        es = ESINK[:, 4 * l:4 * l + 4]
        q0 = pi * 128
        both6 = [("ps", 6, 0), ("ps", 6, 1)]
        both7 = [("ps", 7, 0), ("ps", 7, 1)]
        C.op("dve", TT(v4(RD[:, :]), v4(PS[7][:, :]), es.unsqueeze(2).to_broadcast([128, 4, 128]), ALU.add), r=both7 + ["ESINK"], w=["RD"])
        C.op("act", ACT(RD[:, :], RD[:, :], AF.Ln), r=["RD"], w=["RD"])
        C.op("act", ACT(RD[:, :], RD[:, :], AF.Exp, scale=-1.0), r=["RD"], w=["RD"])
        C.op("dve", TT(AT[:, :, q0:q0 + 128], v4(PS[6][:, :]), v4(RD[:, :]), ALU.mult), r=both6 + ["RD"], w=["AT"])

    def ch_attn_sample(C, t, l):
        QT, AT = BQ[0], BQ[1]
        es = ESINK[:, 4 * l:4 * l + 4]
        v16 = lambda ap: ap.rearrange("p (a b) -> p a b", a=4)
        for g in range(2):
            pr = slice(g * 64, (g + 1) * 64)
            qrhs = QT[pr, :, S5:S5 + 16]
            tpk = (g * 64, 0)
            tpo = (0, g * 64)
            C.op("pe", MM(PS[3][:, 0:64], CKT[pr, l, :], qrhs, tp=tpk), r=["CKT"] + QTALL, w=[("ps", 3)])
            C.op("pe", MM(PS[4][0:16, 0:64], KT[pr, 2304:2320], qrhs, tp=tpk), r=["KT"] + QTALL, w=[("ps", 4)])
            C.op("act", ACT(PAB[:, g, 0, 0:64], PS[3][:, 0:64], AF.Exp, scale=0.125), r=[("ps", 3)], w=["PA%d" % g])
            C.op("act", ACT(PAB[0:16, g, 1, 0:64], PS[4][0:16, 0:64], AF.Exp, scale=0.125), r=[("ps", 4)], w=["PB%d" % g])
            ins = [MM(PS[6][pr, 0:64], CV[:, l, pr], PAB[:, g, 0, 0:64], start=True, stop=False, tp=tpo),
                   MM(PS[6][pr, 0:64], VB[0:16, 18, pr], PAB[0:16, g, 1, 0:64], start=False, stop=True, tp=tpo),
                   MM(PS[7][pr, 0:64], ONESK, PAB[:, g, 0, 0:64], start=True, stop=False, tp=tpo),
                   MM(PS[7][pr, 0:64], ONESK[0:16, :], PAB[0:16, g, 1, 0:64], start=False, stop=True, tp=tpo)]
            C.op("pe", ins, r=[("VB", 18), "CV", "CB", "PA%d" % g, "PB%d" % g], w=[("ps", 6, g), ("ps", 7, g)])
        C.op("dve", TT(v16(RD[:, 0:64]), v16(PS[7][:, 0:64]), es.unsqueeze(2).to_broadcast([128, 4, 16]), ALU.add),
             r=[("ps", 7, 0), ("ps", 7, 1), "ESINK"], w=["RD"])
        C.op("act", ACT(RD[:, 0:64], RD[:, 0:64], AF.Ln), r=["RD"], w=["RD"])
        C.op("act", ACT(RD[:, 0:64], RD[:, 0:64], AF.Exp, scale=-1.0), r=["RD"], w=["RD"])
        C.op("dve", TT(AT[:, :, S5:S5 + 16], v16(PS[6][:, 0:64]), v16(RD[:, 0:64]), ALU.mult), r=[("ps", 6, 0), ("ps", 6, 1), "RD"], w=["AT"])

    def ch_attention(C, t, l):
        got = sorted(C.alloc("st") for _ in range(4))
        assert got == [3, 4, 6, 7], got
        setB = (C.alloc("proj"), C.alloc("proj"))
        sets = [(3, 4), setB]
        units = [attn_unit(t, l, pi, g, sets[g]) for pi in range(nsub(t)) for g in range(2)]

        def emit(lst):
            for eng, insts, r, w in lst:
                C.op(eng, insts, r=r, w=w)
        emit(units[0][0])
        for u in range(len(units)):
            if u + 1 < len(units):
                emit(units[u + 1][0])
            emit(units[u][1])
            if u % 2 == 0 and u >= 2:
                ch_attn_norm(C, t, l, u // 2 - 1)
            emit(units[u][2])
        ch_attn_norm(C, t, l, len(units) // 2 - 1)
        if t == 5:
            ch_attn_sample(C, t, l)

    def ch_pools(C, t, l):
        st, lt, lc0, N = tile_info(t)
        E = 15 + N
        D, PL = BQ[2], BQ[3]
        fss = [C.alloc("fs"), C.alloc("fs"), C.alloc("fs")]
        b = C.peek("proj")
        if t == 5:
            stg = C.alloc("stg")
        for g in range(4):
            bufs = fss[0:2]
            src = None
            for k in range(1, g + 2):
                s_ = 1 << (k - 1)
                lo = (1 << k) - 1
                dst = bufs[(k - 1) % 2]
                if k == 1:
                    C.op("pool", TT(FS[dst][:, lo:E], U[:, g, lo:E], U[:, g, lo - s_:E - s_], ALU.add), r=[("U", g), ("Uc", g)], w=["FS%d" % dst])
                else:
                    C.op("pool", TT(FS[dst][:, lo:E], FS[src][:, lo:E], FS[src][:, lo - s_:E - s_], ALU.add), r=["FS%d" % src], w=["FS%d" % dst])
                src = dst
            wv = float(1 << (g + 1))
            C.op("dve", STT(D[:, g, 0:N], FS[src][:, 15:E], 1.0 / wv, U[:, g, 15:E], ALU.mult, ALU.subtract), r=["FS%d" % src, ("U", g)], w=[("D", g)])
            if t == 0:
                f3 = fss[2]
                C.op("dve", TT(FS[f3][:, 0:16], FS[src][:, 15:31], INVC[:, g * 16:(g + 1) * 16], ALU.mult), r=["FS%d" % src, "CF"], w=["FS%d" % f3])
                C.op("dve", TT(D[:, g, 0:16], FS[f3][:, 0:16], U[:, g, 15:31], ALU.subtract), r=["FS%d" % f3, ("U", g)], w=[("D", g)])
            C.hold()
            C.op("pe", MM(PS[b][:, 0:N], WPOOL[:, g, :], D[:, g, 0:N]), r=["WPOOL", ("D", g)], w=[("ps", b)])
            C.op("act", ACT(PL[:, g, 0:N], PS[b][:, 0:N], AF.Identity, scale=sm(l, O_PSC + g)), r=[("ps", b), "SMALL"], w=[("PL", g)])
            C.unhold()
        if t == 5:
            for c0, dst_ in ((W5, opp), (S5 + 16, ops_)):
                C.hold()
                C.op("pe", [TR(PS[b][0:15, gg * 128:(gg + 1) * 128], U[:, gg, c0:c0 + 15], IDENT) for gg in range(4)],
                     r=[("U", gg) for gg in range(4)] + ["CF"], w=[("ps", b)])
                C.op("act", ACT(STG[stg][0:15, 0:512], PS[b][0:15, 0:512], AF.Copy), r=[("ps", b)], w=["STG%d" % stg])
                C.unhold()
                C.dma("sp", dst_[l, :, :], STG[stg][0:15, 0:512], r=["STG%d" % stg])

    def ch_wout(C, t, mo):
        st, lt, lc0, N = tile_info(t)
        AT, PL = BQ[1], BQ[3]
        b = C.alloc("proj")
        C.op("pe", [MM(PS[b][:, 0:N], WOUT[:, kc, mo * 128:(mo + 1) * 128], (AT[:, kc, 0:N] if kc < 4 else PL[:, kc - 4, 0:N]),
                       start=(kc == 0), stop=(kc == 7)) for kc in range(8)], r=["WOUT", "AT"] + [("PL", g) for g in range(4)], w=[("ps", b)])
        C.op("dve", TT(XT[:, mo, lc0:lc0 + N], PS[b][:, 0:N], XT[:, mo, lc0:lc0 + N], ALU.add), r=[("ps", b), ("XT", lt, mo)], w=[("XT", lt, mo)])

    def ch_prologue(C, t, l):
        ch_rmsnorm(C, t, l * SM_L + O_G1)

    def mixer_tile(t, l, prologue_done, next_t, extra_tail=(), defer_wout=False, pre_wouts=()):
        st, lt, lc0, N = tile_info(t)
        if not prologue_done:
            flush([lambda C: ch_prologue(C, t, l)])
        P.dma("sp", ROPE[:, :, 0:N], rope[:, :, t, 0:N], w=["ROPE"])
        F = lambda fn, *a: (lambda C: fn(C, *a))
        vs = [F(ch_v, t, l, sub) for sub in range(nsub(t) + (1 if t == 5 else 0))]
        us = [F(ch_u, t, l, g) for g in range(4)]
        qs = [F(ch_qk, t, l, j) for j in (4, 0, 1, 2, 3)]
        order = [qs[0], qs[1], vs[0], qs[2], vs[1], qs[3], vs[2], qs[4]] + vs[3:] + [us[0], us[1], us[2], us[3]]
        if pre_wouts:
            merged, pw = [], list(pre_wouts)
            for ch_ in order:
                merged.append(ch_)
                if pw:
                    merged.append(pw.pop(0))
            order = merged + pw
        flush(order, width=4)
        tail = [F(ch_attention, t, l), F(ch_pools, t, l)]
        if next_t is not None:
            tail.append(F(ch_prologue, next_t, l))
        tail += list(extra_tail)
        flush(tail, width=3)
        wouts = [F(ch_wout, t, mo) for mo in range(8)]
        if defer_wout:
            return wouts
        flush(wouts, width=2)

    def out_y(t):
        st, lt, lc0, N = tile_info(t)
        wt = nsub(t) * 128
        for h in range(4):
            P.dma("sp", yp[2 * h:2 * h + 2, :, t * W:t * W + wt].rearrange("k p t -> p k t"), XT[:, 2 * h:2 * h + 2, lc0:lc0 + wt],
                  r=[("XT", lt, kc) for kc in range(2 * h, 2 * h + 2)])
        if t == 5:
            P.dma("sp", ys[:, :, :].rearrange("k p t -> p k t"), XT[:, :, LS:LS + 16], r=[("XT", 2, kc) for kc in range(8)])
        if t < 3:
            load_x(t + 3)

    def ffn_up(C, t, l, grp, slots, mi, only=None):
        _, lt, lc0, N = tile_info(t)
        M = BQ[mi]
        Mname = "BQM%d" % mi
        for si, s in enumerate(grp):
            if only is not None and si not in only:
                continue
            slot = slots[si]
            res = []
            for half in range(2):
                hb = nxt("h", [0, 1, 2, 3])
                c = s + 22 * half
                C.op("pe", [MM(PS[hb][:, 0:N + 2], WUP[:, slot, kc, half * 128:(half + 1) * 128], XN[:, kc, lc0:lc0 + N + 2],
                               start=(kc == 0), stop=(kc == 7)) for kc in range(8)],
                     r=[("WS", slot), ("XN", lt), ("XN", max(lt - 1, 0)), "XNc"], w=[("ps", hb)])
                a = nxt("fs", FSN)
                C.op("act", ACT(FS[a][:, 0:N], PS[hb][:, 2:N + 2], AF.Identity, bias=sm(l, O_CB + c), scale=sm(l, O_CW + 88 + c)),
                     r=[("ps", hb), "SMALL"], w=["FS%d" % a])
                for j in (1, 0):
                    C.op("dve", STT(FS[a][:, 0:N], PS[hb][:, j:N + j], sm(l, O_CW + 44 * j + c), FS[a][:, 0:N], ALU.mult, ALU.add),
                         r=[("ps", hb), "FS%d" % a, "SMALL"], w=["FS%d" % a])
                if t == 5:
                    C.op("dve", TT(FS[a][:, S5:S5 + 2], FS[a][:, S5:S5 + 2], CORR[:, l, c, :], ALU.add), r=["FS%d" % a, "CORR"], w=["FS%d" % a])
                    C.op("act", ACT(HL[:, c, :].rearrange("p (a b) -> p a b", a=2),
                                    PS[hb][:, W5:W5 + 64].rearrange("p (a b) -> p a b", a=2)[:, :, 0:2], AF.Copy), r=[("ps", hb)], w=["HL"])
                res.append(a)
            ag, av = res
            sg = nxt("fs", FSN)
            C.op("act", ACT(FS[sg][:, 0:N], FS[ag][:, 0:N], AF.Silu), r=["FS%d" % ag], w=["FS%d" % sg])
            C.op("pool", TT(M[:, si, 0:N], FS[sg][:, 0:N], FS[av][:, 0:N], ALU.mult), r=["FS%d" % sg, "FS%d" % av], w=[Mname])

    def ffn_down(C, t, l, grp, slots, mi, mos):
        _, lt, lc0, N = tile_info(t)
        M = BQ[mi]
        Mname = "BQM%d" % mi
        for mo in mos:
            yb = nxt("y", [4, 5, 6, 7])
            C.op("pe", [MM(PS[yb][:, 0:N], WDN[:, slots[si], mo * 128:(mo + 1) * 128], M[:, si, 0:N], start=(si == 0), stop=(si == len(grp) - 1))
                        for si in range(len(grp))], r=[("WS", sl) for sl in slots] + [Mname], w=[("ps", yb)])
            if mo % 2 == 0:
                C.op("dve", TT(XT[:, mo, lc0:lc0 + N], PS[yb][:, 0:N], XT[:, mo, lc0:lc0 + N], ALU.add), r=[("ps", yb), ("XT", lt, mo)], w=[("XT", lt, mo)])
            else:
                ya = nxt("fs", FSN)
                C.op("act", ACT(FS[ya][:, 0:N], PS[yb][:, 0:N], AF.Copy), r=[("ps", yb)], w=["FS%d" % ya])
                C.op("pool", TT(XT[:, mo, lc0:lc0 + N], FS[ya][:, 0:N], XT[:, mo, lc0:lc0 + N], ALU.add), r=["FS%d" % ya, ("XT", lt, mo)], w=[("XT", lt, mo)])

    def ffn_pass(st, l, pidx):
        tiles = [3 * st + i for i in range(3)]
        flush([lambda C: ch_rmsnorm(C, tiles[2], l * SM_L + O_G2)])
        if st == 0:
            P.op("dve", CP(CARRY[:, l, :, :], XN[:, :, 1152:1154]), r=[("XN", 2)], w=["CARRY%d" % l])
        else:
            P.op("dve", CP(XN[:, :, 0:2], CARRY[:, l, :, :]), r=["CARRY%d" % l], w=["XNc"])
        units = [(gi, grp, t) for gi, grp in enumerate(GROUPS) for t in tiles]

        def up(ui, only=None):
            gi, grp, t = units[ui]
            slots = [(pidx * 22 + s) % RING for s in grp]
            flush([lambda C: ffn_up(C, t, l, grp, slots, 1 + (ui % 2), only)])

        def down(ui, mos):
            gi, grp, t = units[ui]
            slots = [(pidx * 22 + s) % RING for s in grp]
            flush([lambda C: ffn_down(C, t, l, grp, slots, 1 + (ui % 2), mos)])

        MOS = [[0, 1, 2], [3, 4, 5], [6, 7]]
        up(0)
        for ui, (gi, grp, t) in enumerate(units):
            nslab = len(units[ui + 1][1]) if ui + 1 < len(units) else 0
            for part in range(3):
                if part < nslab:
                    up(ui + 1, only=[part])
                down(ui, MOS[part])
            if gi == len(GROUPS) - 1 and l == 1:
                out_y(t)
            if t == tiles[-1]:
                issue_slabs(pidx * 22 + grp[-1] + 1 + RING)
        if st == 1:
            b = nxt("y", [4, 5, 6, 7])
            stg = nxt("stg", [0, 1])
            P.op("pe", [TR(PS[b][0:44, r_ * 128:(r_ + 1) * 128], HL[:, :, r_], IDENT) for r_ in range(4)], r=["HL", "CF"], w=[("ps", b)])
            P.op("act", ACT(STG[stg][0:44, 0:512], PS[b][0:44, 0:512], AF.Copy), r=[("ps", b)], w=["STG%d" % stg])
            for r_ in range(4):
                dst = (ocp if r_ < 2 else ocs)[l, r_ % 2, :].rearrange("(c p) -> c p", p=128)
                P.dma("sp", dst, STG[stg][0:44, r_ * 128:(r_ + 1) * 128], r=["STG%d" % stg])

    for l in range(2):
        stg = nxt("stg", [0, 1])
        P.dma("sp", STG[stg][:, 0:128], ck[l, :, :], w=["STG%d" % stg])
        b = nxt("proj", [0, 1, 2])
        P.op("pe", TR(PS[b][:, 0:128], STG[stg][:, 0:128], IDENT), r=["STG%d" % stg, "CF"], w=[("ps", b)])
        P.op("act", ACT(CKT[:, l, :], PS[b][:, 0:128], AF.Copy), r=[("ps", b)], w=["CKT"])

    load_mixer_weights(0, gate=True)
    pidx = 0
    for st in range(2):
        for l in range(2):
            if st == 1:
                P.op("dve", CP(KT[:, 1024:1152], KC[:, l, :]), r=["KC"], w=["KT"])
                P.op("dve", CP(VB[:, 8, :], VC[:, l, :]), r=["VC"], w=[("VB", 8)])
            tl = list(range(3 * st, 3 * st + 3))
            if l == 0 and st == 0:
                load_x(1, after=["WIN"])
                load_x(2, after=["WIN"])
            for i, t in enumerate(tl):
                if st == 1 and l == 0 and i == 1:
                    P.op("dve", MS(XT[:, :, LG:LG + 16], 0.0), w=[("XT", 2, kc) for kc in range(8)])
                if i < 2:
                    pend = mixer_tile(t, l, prologue_done=(i > 0), next_t=tl[i + 1], defer_wout=True, pre_wouts=(pend if i > 0 else ()))
                else:
                    extra = [(lambda C, t_=t_: ch_rmsnorm(C, t_, l * SM_L + O_G2)) for t_ in tl[0:2]]
                    mixer_tile(t, l, prologue_done=True, next_t=None, extra_tail=extra, pre_wouts=pend)
                if pidx == 0 and i == 0:
                    issue_slabs(RING)
            if st == 0:
                P.op("dve", CP(KC[:, l, :], KT[:, 1024:1152]), r=["KT"], w=["KC"])
                P.op("dve", CP(VC[:, l, :], VB[:, 8, :]), r=[("VB", 8)], w=["VC"])
            if pidx + 1 < 4:
                load_mixer_weights((pidx + 1) % 2)
            ffn_pass(st, l, pidx)
            pidx += 1
    P.finish()
    return nc


def _prep(inputs):
    f = lambda a: np.ascontiguousarray(np.asarray(a, dtype=np.float32))
    x_prompt = f(inputs["x_prompt"]); x_sample = f(inputs["x_sample"])
    cache_k = f(inputs["cache_k"]); cache_v = f(inputs["cache_v"])
    state_pool = f(inputs["state_pool"]); state_conv = f(inputs["state_conv"])
    w_in = f(inputs["w_in"]); w_out = f(inputs["w_out"]); w_pool = f(inputs["w_pool"])
    w_up = f(inputs["w_up"]); w_down = f(inputs["w_down"])
    qperm = np.concatenate([np.concatenate([np.arange(j * 64, j * 64 + 64), np.arange((4 + j) * 64, (4 + j) * 64 + 64)]) for j in range(4)])
    colperm = np.concatenate([qperm, np.arange(512, 640), np.arange(768, 1280), np.arange(640, 768)])
    win = np.ascontiguousarray(w_in[:, :, colperm].reshape(2, 8, 128, 1280).transpose(0, 2, 1, 3))
    rowperm = np.concatenate([qperm, np.arange(512, 1024)])
    wout = np.ascontiguousarray(w_out[:, rowperm, :].reshape(2, 8, 128, 1024).transpose(0, 2, 1, 3))
    wpool = np.ascontiguousarray(w_pool.transpose(0, 2, 1, 3))
    wu = w_up.reshape(2, 8, 128, 2, 22, 128)
    wup = np.ascontiguousarray(wu.transpose(0, 4, 2, 1, 3, 5).reshape(2, 22, 128, 8, 256))
    wdn = np.ascontiguousarray(w_down.reshape(2, 22, 128, 1024))
    small = np.zeros((128, 405), np.float32)
    for l in range(2):
        o = l * SM_L
        small[:, o + O_G1:o + O_G1 + 8] = f(inputs["norm_mix"])[l].reshape(8, 128).T
        small[:, o + O_G2:o + O_G2 + 8] = f(inputs["norm_ffn"])[l].reshape(8, 128).T
        small[:, o + O_GQ] = np.tile(f(inputs["q_norm"])[l], 2)
        small[:, o + O_GK] = np.tile(f(inputs["k_norm"])[l], 2)
        sk = f(inputs["attn_sinks"])[l].reshape(2, 4)
        small[:, o + O_SINK:o + O_SINK + 4] = np.repeat(sk, 64, axis=0)
        small[:, o + O_PSC:o + O_PSC + 4] = f(inputs["pool_scale"])[l].reshape(4, 128).T
        small[:, o + O_CW:o + O_CW + 132] = f(inputs["conv_w"])[l].reshape(3, 44, 128).transpose(2, 0, 1).reshape(128, 132)
        small[:, o + O_CB:o + O_CB + 44] = f(inputs["conv_b"])[l].reshape(44, 128).T
    small[:, O_EPS] = EPS
    constf = np.zeros((128, 192), np.float32)
    constf[:, 0:128] = np.eye(128, dtype=np.float32)
    for g in range(4):
        wv = 2 << g
        constf[:, 128 + g * 16:128 + (g + 1) * 16] = 1.0 / np.minimum(np.arange(16) + 1, wv).astype(np.float32)
    constb = np.zeros((128, 448), np.float32)
    constb[:, 0:128] = 1.0 / 1024.0
    constb[0:64, 128:192] = 1.0 / 64.0
    constb[64:128, 192:256] = 1.0 / 64.0
    d = np.arange(128) % 64
    for m in range(128):
        if d[m] < 8:
            constb[m + 8, 256 + m] = 1.0
        elif d[m] < 16:
            constb[m - 8, 256 + m] = 1.0
    constb[:, 384:448] = 1.0
    inv = (500000.0 ** (-np.arange(0, 16, 2, dtype=np.float32) / 16.0)).astype(np.float32)
    in_maps = []
    for c in range(NCORES):
        s, hf = divmod(c, 2)
        t0 = 0 if hf == 0 else 4096 - TCORE
        pos = np.zeros((NTILE, NX), np.float32)
        for t in range(NTILE):
            pos[t, 0:W] = t0 + t * W + np.arange(W)
        pos[5, S5:S5 + 16] = 4096 + np.arange(16)
        ang = pos[None, :, :] * inv[:, None, None]
        cosv, sinv = np.cos(ang).astype(np.float32), np.sin(ang).astype(np.float32)
        rope = np.zeros((128, 2, NTILE, NX), np.float32)
        rope[:, 0] = 1.0
        for p in range(128):
            dd = p % 64
            if dd < 8:
                rope[p, 0] = cosv[dd]; rope[p, 1] = -sinv[dd]
            elif dd < 16:
                rope[p, 0] = cosv[dd - 8]; rope[p, 1] = sinv[dd - 8]
        in_maps.append({
            "xp": np.ascontiguousarray(x_prompt[s, t0:t0 + TCORE, :].T).reshape(8, 128, TCORE),
            "xs": np.ascontiguousarray(x_sample[c].T).reshape(8, 128, 16),
            "ck": np.ascontiguousarray(cache_k[:, c].reshape(2, 128, 128)),
            "cv": np.ascontiguousarray(cache_v[:, c].reshape(2, 128, 128)),
            "spool": np.ascontiguousarray(state_pool[:, c]),
            "sconv": np.ascontiguousarray(state_conv[:, c]),
            "win": win, "wout": wout, "wpool": wpool, "wup": wup, "wdn": wdn,
            "small": small, "constf": constf, "constb": constb, "rope": rope,
        })
    return in_maps


_NC_CACHE = {}


def kernel(**inputs):
    in_maps = _prep(inputs)
    if "nc" not in _NC_CACHE:
        _NC_CACHE["nc"] = build_nc()
    nc = _NC_CACHE["nc"]
    res = run_bass_kernel_spmd(nc, in_maps, core_ids=list(range(NCORES)))
    R = res.results
    y_prompt = np.zeros((4, 4096, 1024), np.float32)
    for s in range(4):
        y_prompt[s, 0:TCORE] = R[2 * s]["yp"].reshape(1024, TCORE).T
        y_prompt[s, TCORE:4096] = R[2 * s + 1]["yp"].reshape(1024, TCORE).T[2 * TCORE - 4096:]
    y_sample = np.stack([R[c]["ys"].reshape(1024, 16).T for c in range(8)])
    pk = lambda name, shape: np.stack([np.stack([R[2 * s + 1][name][l] for s in range(4)]) for l in range(2)]).reshape(shape)
    sk = lambda name, shape: np.stack([np.stack([R[c][name][l] for c in range(8)]) for l in range(2)]).reshape(shape)
    return (y_prompt, y_sample,
            pk("okp", (2, 4, 128, 2, 64)), pk("ovp", (2, 4, 128, 2, 64)), pk("opp", (2, 4, 15, 512)), pk("ocp", (2, 4, 2, 5632)),
            sk("oks", (2, 8, 128, 2, 64)), sk("ovs", (2, 8, 128, 2, 64)), sk("ops", (2, 8, 15, 512)), sk("ocs", (2, 8, 2, 5632)))
```
